# Optimizing a Trainium2 kernel written in Bass

```python
import jax, jax.numpy as jnp
from jax import lax
import numpy as np

D_MODEL = 1024
BATCH = 8
SEQ = 4096
DEPTH = 1

CHUNK = 64
POOL_WINDOWS = (2, 4, 8, 16)
N_POOL_GROUPS = len(POOL_WINDOWS)
POOL_WIDTH = D_MODEL // 2
POOL_GROUP_DIM = POOL_WIDTH // N_POOL_GROUPS
SSM_HEAD_DIM = 64
SSM_WIDTH = 3 * D_MODEL // 2
SSM_HEADS = SSM_WIDTH // SSM_HEAD_DIM
SSM_GROUPS = 4
SSM_STATE = 128
CONV_WIDTH = 4
MIX_WIDTH = POOL_WIDTH + SSM_WIDTH
CONV_CHANNELS = SSM_WIDTH + 2 * SSM_GROUPS * SSM_STATE
IN_PROJ_WIDTH = POOL_WIDTH + SSM_WIDTH + CONV_CHANNELS + SSM_HEADS
D_FF = 2816
FFN_RES_WEIGHT = 0.5
N_SUBLAYERS = 3
NORM_EPS = 1e-5

kernel_name = "hybrid_pool_ssd_macaron_adaln_block"


def rmsnorm(x, g):
    xf = x.astype(jnp.float32)
    y = xf * lax.rsqrt(jnp.mean(xf * xf, axis=-1, keepdims=True) + NORM_EPS)
    return (y * g.astype(jnp.float32)).astype(x.dtype)


def modulate(h, shift, scale):
    return h * (1.0 + scale[:, None, :]) + shift[:, None, :]


def swiglu(h, w_in, w_out):
    a, b = jnp.split(h @ w_in, 2, axis=-1)
    return (jax.nn.silu(a) * b) @ w_out


def pool_mixer(u, w_pool, pool_scale):
    b, s, _ = u.shape
    ug = u.reshape(b, s, N_POOL_GROUPS, POOL_GROUP_DIM).astype(jnp.float32)
    cs = jnp.cumsum(ug, axis=1)
    t = jnp.arange(s)
    means = []
    for gi, w in enumerate(POOL_WINDOWS):
        csg = cs[:, :, gi]
        lo = jnp.pad(csg, ((0, 0), (w, 0), (0, 0)))[:, :s]
        cnt = jnp.minimum(t + 1, w).astype(jnp.float32)[None, :, None]
        means.append((csg - lo) / cnt)
    pooled = (jnp.stack(means, axis=2) - ug).astype(u.dtype)
    mixed = jnp.einsum('bsgc,gcd->bsgd', pooled, w_pool)
    return mixed.reshape(b, s, POOL_WIDTH) * pool_scale


def causal_depthwise_conv(u, w, bias):
    ch = u.shape[-1]
    y = lax.conv_general_dilated(u, w[:, None, :].astype(u.dtype), window_strides=(1,),
                                 padding=[(CONV_WIDTH - 1, 0)],
                                 dimension_numbers=('NWC', 'WIO', 'NWC'),
                                 feature_group_count=ch)
    return y + bias


def segsum(a):
    L = a.shape[-1]
    cs = jnp.cumsum(a, axis=-1)
    diff = cs[..., :, None] - cs[..., None, :]
    mask = jnp.tril(jnp.ones((L, L), dtype=bool))
    return jnp.where(mask, diff, -jnp.inf)


def ssd_chunked(xs, dt, A, Bm, Cm):
    b, s, h, p = xs.shape
    g, n = Bm.shape[2], Bm.shape[3]
    r = h // g
    c = s // CHUNK
    xd = (xs.astype(jnp.float32) * dt[..., None]).reshape(b, c, CHUNK, g, r, p)
    a = (dt * A).reshape(b, c, CHUNK, g, r)
    a = jnp.moveaxis(a, 2, -1)
    Bc = Bm.astype(jnp.float32).reshape(b, c, CHUNK, g, n)
    Cc = Cm.astype(jnp.float32).reshape(b, c, CHUNK, g, n)
    a_cs = jnp.cumsum(a, axis=-1)
    scores = jnp.einsum('bclgn,bcmgn->bcglm', Cc, Bc)
    M = scores[:, :, :, None] * jnp.exp(segsum(a))
    y_diag = jnp.einsum('bcgrlm,bcmgrp->bclgrp', M, xd)
    decay_states = jnp.moveaxis(jnp.exp(a_cs[..., -1:] - a_cs), -1, 2)
    states = jnp.einsum('bcmgn,bcmgrp->bcgrpn', Bc, xd * decay_states[..., None])
    chunk_decay = jnp.exp(a_cs[..., -1])

    def step(carry, inp):
        st, dec = inp
        return carry * dec[..., None, None] + st, carry

    init = jnp.zeros((b, g, r, p, n), jnp.float32)
    _, prev = lax.scan(step, init, (jnp.moveaxis(states, 1, 0), jnp.moveaxis(chunk_decay, 1, 0)))
    prev = jnp.moveaxis(prev, 0, 1)
    decay_out = jnp.moveaxis(jnp.exp(a_cs), -1, 2)
    y_off = jnp.einsum('bclgn,bcgrpn->bclgrp', Cc, prev) * decay_out[..., None]
    return (y_diag + y_off).reshape(b, s, h, p)


def hybrid_mixer(h, w_in, w_pool, pool_scale, conv_w, conv_b, dt_bias, a_log, d_skip, ssm_norm_g, w_out):
    b, s, _ = h.shape
    proj = h @ w_in
    i0 = POOL_WIDTH
    i1 = i0 + SSM_WIDTH
    i2 = i1 + CONV_CHANNELS
    u_pool, z, xbc, dt_raw = jnp.split(proj, [i0, i1, i2], axis=-1)
    y_pool = pool_mixer(u_pool, w_pool, pool_scale)
    xbc = jax.nn.silu(causal_depthwise_conv(xbc, conv_w, conv_b))
    xs, Bm, Cm = jnp.split(xbc, [SSM_WIDTH, SSM_WIDTH + SSM_GROUPS * SSM_STATE], axis=-1)
    xs = xs.reshape(b, s, SSM_HEADS, SSM_HEAD_DIM)
    Bm = Bm.reshape(b, s, SSM_GROUPS, SSM_STATE)
    Cm = Cm.reshape(b, s, SSM_GROUPS, SSM_STATE)
    dt = jax.nn.softplus(dt_raw.astype(jnp.float32) + dt_bias.astype(jnp.float32))
    A = -jnp.exp(a_log.astype(jnp.float32))
    y = ssd_chunked(xs, dt, A, Bm, Cm) + d_skip.astype(jnp.float32)[:, None] * xs.astype(jnp.float32)
    y = y.reshape(b, s, SSM_WIDTH) * jax.nn.silu(z.astype(jnp.float32))
    yg = y.reshape(b, s, SSM_GROUPS, SSM_WIDTH // SSM_GROUPS)
    yg = yg * lax.rsqrt(jnp.mean(yg * yg, axis=-1, keepdims=True) + NORM_EPS)
    y_ssm = (yg.reshape(b, s, SSM_WIDTH) * ssm_norm_g.astype(jnp.float32)).astype(h.dtype)
    return jnp.concatenate([y_pool, y_ssm], axis=-1) @ w_out


def setup_inputs(seed: int = 0) -> dict:
    key = jax.random.key(seed)
    ks = jax.random.split(key, 24)
    f32 = jnp.float32
    nrm = lambda k, shape, scale: jax.random.normal(k, shape, f32) * scale
    dt0 = jnp.exp(jax.random.uniform(ks[14], (DEPTH, SSM_HEADS), f32, np.log(1e-3), np.log(1e-1)))
    return {
        "x": nrm(ks[0], (BATCH, SEQ, D_MODEL), 1.0),
        "c": nrm(ks[1], (BATCH, D_MODEL), 1.0),
        "w_ada": nrm(ks[2], (DEPTH, D_MODEL, N_SUBLAYERS * 3 * D_MODEL), 0.5 * D_MODEL ** -0.5),
        "b_ada": nrm(ks[3], (DEPTH, N_SUBLAYERS * 3 * D_MODEL), 0.01),
        "norm_g": 1.0 + nrm(ks[4], (DEPTH, N_SUBLAYERS, D_MODEL), 0.02),
        "ffn1_in": nrm(ks[5], (DEPTH, D_MODEL, 2 * D_FF), D_MODEL ** -0.5),
        "ffn1_out": nrm(ks[6], (DEPTH, D_FF, D_MODEL), D_FF ** -0.5),
        "w_in": nrm(ks[7], (DEPTH, D_MODEL, IN_PROJ_WIDTH), D_MODEL ** -0.5),
        "w_pool": nrm(ks[8], (DEPTH, N_POOL_GROUPS, POOL_GROUP_DIM, POOL_GROUP_DIM), POOL_GROUP_DIM ** -0.5),
        "pool_scale": 1.0 + nrm(ks[9], (DEPTH, POOL_WIDTH), 0.02),
        "conv_w": nrm(ks[10], (DEPTH, CONV_WIDTH, CONV_CHANNELS), CONV_WIDTH ** -0.5),
        "conv_b": nrm(ks[11], (DEPTH, CONV_CHANNELS), 0.01),
        "dt_bias": dt0 + jnp.log(-jnp.expm1(-dt0)),
        "a_log": jnp.log(jax.random.uniform(ks[12], (DEPTH, SSM_HEADS), f32, 1.0, 16.0)),
        "d_skip": 1.0 + nrm(ks[13], (DEPTH, SSM_HEADS), 0.1),
        "ssm_norm_g": 1.0 + nrm(ks[15], (DEPTH, SSM_WIDTH), 0.02),
        "w_out": nrm(ks[16], (DEPTH, MIX_WIDTH, D_MODEL), MIX_WIDTH ** -0.5),
        "ffn2_in": nrm(ks[17], (DEPTH, D_MODEL, 2 * D_FF), D_MODEL ** -0.5),
        "ffn2_out": nrm(ks[18], (DEPTH, D_FF, D_MODEL), D_FF ** -0.5),
        "final_g": 1.0 + nrm(ks[19], (D_MODEL,), 0.02),
    }


def reference(x, c, w_ada, b_ada, norm_g, ffn1_in, ffn1_out, w_in, w_pool, pool_scale, conv_w, conv_b,
              dt_bias, a_log, d_skip, ssm_norm_g, w_out, ffn2_in, ffn2_out, final_g):
    bsz = c.shape[0]
    for layer in range(DEPTH):
        mod = (jax.nn.silu(c) @ w_ada[layer] + b_ada[layer]).reshape(bsz, N_SUBLAYERS, 3, D_MODEL)
        shift, scale, gate = mod[:, :, 0], mod[:, :, 1], mod[:, :, 2]
        h = modulate(rmsnorm(x, norm_g[layer, 0]), shift[:, 0], scale[:, 0])
        x = x + FFN_RES_WEIGHT * gate[:, 0, None, :] * swiglu(h, ffn1_in[layer], ffn1_out[layer])
        h = modulate(rmsnorm(x, norm_g[layer, 1]), shift[:, 1], scale[:, 1])
        mix = hybrid_mixer(h, w_in[layer], w_pool[layer], pool_scale[layer], conv_w[layer], conv_b[layer],
                           dt_bias[layer], a_log[layer], d_skip[layer], ssm_norm_g[layer], w_out[layer])
        x = x + gate[:, 1, None, :] * mix
        h = modulate(rmsnorm(x, norm_g[layer, 2]), shift[:, 2], scale[:, 2])
        x = x + FFN_RES_WEIGHT * gate[:, 2, None, :] * swiglu(h, ffn2_in[layer], ffn2_out[layer])
    return rmsnorm(x, final_g)
```

```python
from contextlib import ExitStack
import numpy as np
import concourse.bass as bass
import concourse.mybir as mybir
from concourse.bass_utils import run_bass_kernel_spmd

F32 = mybir.dt.float32
BF16 = mybir.dt.bfloat16
AF = mybir.ActivationFunctionType
ALU = mybir.AluOpType
AX = mybir.AxisListType

D = 1024
KC = 8
S = 4096
T = 512
MT = 4
DFF = 2816
NJ = 22
NH = 24
EPS = 1e-5
NSLOT = 3
SLOT_ELEMS = 4096


class Buf:
    __slots__ = ("name", "lw", "rd", "excl")

    def __init__(self, name, excl=False):
        self.name = name
        self.lw = None
        self.rd = {}
        self.excl = excl


class Prog:
    ENG = ("pe", "act", "dve", "pool", "sp")

    def __init__(self, nc, stack):
        self.nc = nc
        self.stack = stack
        self.streams = {e: [] for e in self.ENG}
        self.sem = {e: stack.enter_context(nc.semaphore("prog_" + e)) for e in self.ENG}
        self.cnt = {e: 0 for e in self.ENG}
        self.waited = {e: {} for e in self.ENG}
        self.dma_sems = {}
        self.dma_cnt = {}

    def new_dma_sem(self, name):
        self.dma_sems[name] = self.stack.enter_context(self.nc.semaphore("dma_" + name))
        self.dma_cnt[name] = 0
        return name

    def _semh(self, key):
        return self.sem[key] if key in self.sem else self.dma_sems[key]

    def _collect(self, eng, reads, writes):
        need = {}

        def add(tok):
            if tok is None:
                return
            k, v = tok
            if k == eng and eng == "pe":
                return
            if need.get(k, 0) < v:
                need[k] = v
        for b in reads:
            add(b.lw)
        for b in writes:
            add(b.lw)
            for k, v in b.rd.items():
                add((k, v))
        out = []
        w = self.waited[eng]
        for k, v in need.items():
            if w.get(k, 0) < v:
                w[k] = v
                out.append((k, v))
        return out

    def op(self, eng, fns, reads=(), writes=()):
        if callable(fns):
            fns = [fns]
        writes = list(writes) + [b for b in reads if b.excl]
        reads = [b for b in reads if not b.excl]
        waits = self._collect(eng, reads, writes)
        self.cnt[eng] += 1
        idx = self.cnt[eng]
        self.streams[eng].append((waits, fns, (eng, 1)))
        for b in reads:
            if b.rd.get(eng, 0) < idx:
                b.rd[eng] = idx
        for b in writes:
            b.lw = (eng, idx)
            b.rd = {}
        return (eng, idx)

    def dma(self, qeng, semname, fn, reads=(), writes=()):
        waits = self._collect(qeng, reads, writes)
        prev = self.dma_cnt[semname]
        if prev > 0 and self.waited[qeng].get(semname, 0) < prev:
            self.waited[qeng][semname] = prev
            waits.append((semname, prev))
        self.dma_cnt[semname] += 16
        val = self.dma_cnt[semname]
        self.streams[qeng].append((waits, [fn], (semname, 16)))
        for b in reads:
            if b.rd.get(semname, 0) < val:
                b.rd[semname] = val
        for b in writes:
            b.lw = (semname, val)
            b.rd = {}
        return (semname, val)

    def wait_all(self, eng, toks):
        waits = []
        for k, v in toks:
            if self.waited[eng].get(k, 0) < v:
                self.waited[eng][k] = v
                waits.append((k, v))
        self.streams[eng].append((waits, [], None))

    def emit(self):
        engmap = {"pe": "tensor", "act": "scalar", "dve": "vector", "pool": "gpsimd", "sp": "sync"}
        with self.nc.Block() as block:
            for e in self.ENG:
                stream = self.streams[e]

                def body(engine, stream=stream):
                    for waits, fns, inc in stream:
                        for k, v in waits:
                            engine.wait_ge(self._semh(k), v)
                        n = len(fns)
                        for i, f in enumerate(fns):
                            ins = f(engine)
                            if i == n - 1 and inc is not None:
                                ins.then_inc(self._semh(inc[0]), inc[1])
                getattr(block, engmap[e])(body)


def build_program(ntiles=8, stage=3):
    nc = bass.Bass("TRN2", target_bir_lowering=False)
    dram_in = lambda name, shape: nc.dram_tensor(name, shape, F32, kind="ExternalInput").ap()
    x_d = dram_in("x", [S, D])
    c_d = dram_in("c", [D])
    wada_d = dram_in("w_ada", [D, 9 * D])
    bada_d = dram_in("b_ada", [9 * D])
    ng_d = dram_in("norm_g", [3, D])
    ffn_in_d = [dram_in("ffn1_in", [D, 2 * DFF]), dram_in("ffn2_in", [D, 2 * DFF])]
    ffn_out_d = [dram_in("ffn1_out", [DFF, D]), dram_in("ffn2_out", [DFF, D])]
    win_d = dram_in("w_in", [D, 4632])
    wpool_d = dram_in("w_pool", [4, 128, 128])
    pscale_d = dram_in("pool_scale", [512])
    convw_d = dram_in("conv_w", [4, 2560])
    convb_d = dram_in("conv_b", [2560])
    dtb_d = dram_in("dt_bias", [NH])
    alog_d = dram_in("a_log", [NH])
    dskip_d = dram_in("d_skip", [NH])
    ssmg_d = dram_in("ssm_norm_g", [1536])
    wout_d = dram_in("w_out", [2048, D])
    fg_d = dram_in("final_g", [D])
    cst_d = dram_in("cst", [128, 576])
    out_d = nc.dram_tensor("out", [S, D], F32, kind="ExternalOutput").ap()

    with ExitStack() as st:
        P = Prog(nc, st)
        sb = lambda name, shape, dt: st.enter_context(nc.sbuf_tensor("sb_" + name, shape, dt))

        units = {}
        order_pre = []

        def add_unit(name, src, A, Bc):
            scr = nc.dram_tensor("scr_" + name, [128, A, Bc], BF16).ap()
            units[name] = dict(scr=scr, src=src, A=A, Bc=Bc, buf=Buf("scr_" + name))
            order_pre.append(name)

        def add_ffn_units(f):
            wv = ffn_in_d[f].rearrange("(kc p) n -> p kc n", p=128)
            for g in range(6):
                nc_ = 512 if g < 5 else 256
                add_unit("f%d_a%d" % (f, g), wv[:, :, g * 512:g * 512 + nc_], 8, nc_)
                add_unit("f%d_b%d" % (f, g), wv[:, :, DFF + g * 512:DFF + g * 512 + nc_], 8, nc_)
            wo = ffn_out_d[f].rearrange("(k p) n -> p k n", p=128)
            for g in range(6):
                nk = 4 if g < 5 else 2
                add_unit("f%d_o%d" % (f, g), wo[:, g * 4:g * 4 + nk, :], nk, 1024)

        add_ffn_units(0)
        wiv = win_d.rearrange("(kc p) n -> p kc n", p=128)
        add_unit("m_up", wiv[:, :, 0:512], 8, 512)
        add_unit("m_B", wiv[:, :, 3584:4096], 8, 512)
        add_unit("m_C", wiv[:, :, 4096:4608], 8, 512)
        add_unit("m_dt", wiv[:, :, 4608:4632], 8, 24)
        for i in range(3):
            add_unit("m_xs%d" % i, wiv[:, :, 2048 + i * 512:2048 + (i + 1) * 512], 8, 512)
        for i in range(3):
            add_unit("m_z%d" % i, wiv[:, :, 512 + i * 512:512 + (i + 1) * 512], 8, 512)
        wov = wout_d.rearrange("(k p) n -> p k n", p=128)
        for g in range(4):
            add_unit("m_o%d" % g, wov[:, g * 4:(g + 1) * 4, :], 4, 1024)
        add_ffn_units(1)

        cst = sb("cst", [128, 576], F32)
        identf = cst[:, 0:128]
        tri = cst[:, 128:256]
        ustr = cst[:, 256:384]
        ones = cst[:, 384:512]
        invcnt = cst[:, 512:576]
        ident = sb("ident", [128, 128], BF16)
        vecs1 = sb("vecs1", [128, 128], F32)
        vecs2 = sb("vecs2", [128, 20], F32)
        modv = sb("modv", [128, 3, 3, 8], F32)
        gs = sb("gs", [128, 3, 8], F32)
        fg_bc = sb("fg_bc", [128, 1024], F32)
        dtb_bc = sb("dtb_bc", [128, NH], F32)
        A_bc = sb("A_bc", [128, NH], F32)
        dsk_bc = sb("dsk_bc", [128, NH], F32)
        wpool = sb("wpool", [128, 4, 128], BF16)
        sc = sb("sc", [128, 8], F32)
        sc_b = sb("sc_b", [128, 8, 128], BF16)
        xts = [sb("xt%d" % i, [128, MT, D], F32) for i in range(2)]
        yn = sb("yn", [128, MT, 1536], BF16)
        xn = yn[:].rearrange("p m f -> p (m f)")[:, 0:MT * D].rearrange("p (m f) -> p m f", m=MT)
        hT = sb("hT", [128, KC, T], BF16)
        gT = sb("gT", [128, NJ, T], BF16)
        ring = [sb("ring%d" % i, [128, SLOT_ELEMS], BF16) for i in range(NSLOT)]
        scr4 = [sb("scr%d" % i, [128, 528], F32) for i in range(4)]
        stat = sb("stat", [128, 16], F32)
        upool = sb("upool", [128, 4, 528], F32)
        pooledT = sb("pooledT", [128, 4, T], BF16)
        halo = sb("halo", [128, 20, 3], F32)
        sz = sb("sz", [128, MT, 1536], BF16)
        gate_bc = sz[:].rearrange("p m f -> p (m f)").bitcast(F32).rearrange("p (s d) -> p s d", s=3)
        dtc = sb("dtc", [128, 8, MT * NH], F32)
        rhsD = [sb("rhsD%d" % i, [128, 2, 768], BF16) for i in range(2)]
        avs = sb("avs", [128, 2, MT * NH], BF16)
        trib = sb("trib", [128, 2, 128], BF16)
        expD = [sb("expD%d" % i, [128, 6, 128], F32) for i in range(2)]
        MTt = [sb("MT%d" % i, [128, 6, 128], BF16) for i in range(2)]
        smk = [sb("smk%d" % i, [128, 4, 128], F32) for i in range(2)]
        xd = [sb("xd%d" % i, [128, 6, 64], BF16) for i in range(2)]
        xdd = [sb("xdd%d" % i, [128, 6, 64], BF16) for i in range(2)]
        xsD = [sb("xsD%d" % i, [128, 6, 64], F32) for i in range(2)]
        btok = [sb("btok%d" % i, [128, 128], BF16) for i in range(2)]
        stat2 = sb("stat2", [128, 2, 4], F32)
        yt = sb("yt", [128, NH, 64], F32)
        t1 = sb("t1", [128, 6, 64], F32)
        state = sb("state", [128, NH, 64], F32)
        state_bf = sb("state_bf", [128, NH, 64], BF16)
        ymT = sb("ymT", [128, 16, T], BF16)
        PS = st.enter_context(nc.psum_tensor("PS", [128, 4096], F32))

        def bank(b, n=512):
            return PS[:, b * 512:b * 512 + n]

        def bank_bf(hb):
            return PS[:, hb * 256:(hb + 1) * 256].bitcast(BF16)

        b_cst = Buf("cst"); b_ident = Buf("ident"); b_vecs = Buf("vecs"); b_modv = Buf("modv"); b_gs = Buf("gs")
        b_gate = [Buf("gate%d" % i) for i in range(3)]
        b_small = Buf("small")
        b_wpool = Buf("wpool"); b_sc = Buf("sc")
        b_xs = [[Buf("x%d_%d" % (i, m)) for m in range(MT)] for i in range(2)]
        b_xn = [Buf("xn%d" % m) for m in range(MT)]
        b_yn = [[Buf("yn%d_%d" % (m, g)) for g in range(4)] for m in range(MT)]
        all_yn = [b for row in b_yn for b in row]
        b_hT = [Buf("hT%d" % k) for k in range(KC)]
        b_gT = [Buf("gT%d" % j) for j in range(NJ)]
        b_ring = [Buf("ring%d" % i) for i in range(NSLOT)]
        b_scr = [Buf("scr%d" % i) for i in range(4)]
        b_stat = Buf("stat")
        b_up = [Buf("up%d" % g) for g in range(4)]
        b_pooled = [Buf("pooled%d" % g) for g in range(4)]
        b_halo = Buf("halo")
        b_sz = [Buf("sz%d" % m) for m in range(MT)]
        b_dtc = Buf("dtc"); b_avs = Buf("avs")
        b_rhsD = [Buf("rhsD0"), Buf("rhsD1")]; b_expD = [Buf("expD0"), Buf("expD1")]
        b_MT = [Buf("MT0"), Buf("MT1")]; b_smk = [Buf("smk0"), Buf("smk1")]
        b_xd = [Buf("xd0"), Buf("xd1")]; b_xdd = [Buf("xdd0"), Buf("xdd1")]; b_xsD = [Buf("xsD0"), Buf("xsD1")]
        b_btok = [Buf("btok0"), Buf("btok1")]; b_stat2 = [Buf("stat2_0"), Buf("stat2_1")]
        b_yt = [Buf("yt%d" % g) for g in range(4)]
        b_t1 = Buf("t1")
        b_state = [Buf("st%d" % g) for g in range(4)]
        b_statebf = [Buf("stbf%d" % g) for g in range(4)]
        b_ymT = [Buf("ymT%d" % k) for k in range(16)]
        b_bank = [Buf("bank%d" % i, excl=True) for i in range(8)]
        b_ps = [b_bank[i // 2] for i in range(16)]
        b_out = Buf("out")
        b_outm = [Buf("out%d" % m) for m in range(MT)]

        def pb(b):
            return [b_bank[b]]

        s_misc = P.new_dma_sem("misc")
        s_x = P.new_dma_sem("x")
        s_out = P.new_dma_sem("out")
        s_ring = [P.new_dma_sem("ring%d" % i) for i in range(NSLOT)]
        s_bb = [P.new_dma_sem("bb0"), P.new_dma_sem("bb1")]
        s_wada = [P.new_dma_sem("wada%d" % i) for i in range(NSLOT)]

        ring_pos = [0]

        def slot_view(si, A, Bc):
            return ring[si][:, 0:A * Bc].rearrange("p (a b) -> p a b", a=A)

        def load_unit(name):
            u = units[name]
            si = ring_pos[0] % NSLOT
            ring_pos[0] += 1
            v = slot_view(si, u["A"], u["Bc"])
            P.dma("sp", s_ring[si], lambda e, v=v, u=u: e.dma_start(out=v, in_=u["scr"]),
                  reads=[u["buf"]], writes=[b_ring[si]])
            return v, b_ring[si]

        P.dma("sp", s_misc, lambda e: e.dma_start(out=cst[:], in_=cst_d), writes=[b_cst])
        vrows = sb("vrows", [128, 128], F32)
        vrows2 = sb("vrows2", [20, 128], F32)
        b_vrows = Buf("vrows")
        P.dma("sp", s_misc, lambda e: e.dma_start(out=vrows[0:24, :], in_=ng_d.rearrange("s (kc p) -> (s kc) p", p=128)), writes=[b_vrows])
        P.dma("sp", s_misc, lambda e: e.dma_start(out=vrows[24:104, :], in_=convw_d.rearrange("k (cc p) -> (k cc) p", p=128)), writes=[b_vrows])
        P.dma("sp", s_misc, lambda e: e.dma_start(out=vrows[104:124, :], in_=convb_d.rearrange("(cc p) -> cc p", p=128)), writes=[b_vrows])
        P.dma("sp", s_misc, lambda e: e.dma_start(out=vrows[124:128, :], in_=pscale_d.rearrange("(g p) -> g p", p=128)), writes=[b_vrows])
        P.dma("sp", s_misc, lambda e: e.dma_start(out=vrows2[0:12, :], in_=ssmg_d.rearrange("(cc p) -> cc p", p=128)), writes=[b_vrows])
        P.dma("sp", s_misc, lambda e: e.dma_start(out=vrows2[12:20, :], in_=c_d.rearrange("(kc p) -> kc p", p=128)), writes=[b_vrows])
        P.dma("sp", s_misc, lambda e: e.dma_start(out=dtb_bc[:], in_=dtb_d.unsqueeze(0).broadcast_to([128, NH])), writes=[b_small])
        P.dma("sp", s_misc, lambda e: e.dma_start(out=A_bc[:], in_=alog_d.unsqueeze(0).broadcast_to([128, NH])), writes=[b_small])
        P.dma("sp", s_misc, lambda e: e.dma_start(out=dsk_bc[:], in_=dskip_d.unsqueeze(0).broadcast_to([128, NH])), writes=[b_small])
        P.dma("sp", s_misc, lambda e: e.dma_start(out=fg_bc[:], in_=fg_d.unsqueeze(0).broadcast_to([128, D])), writes=[b_small])
        s_wp = P.new_dma_sem("wp")
        P.dma("pool", s_wp, lambda e: e.dma_start(out=wpool[:], in_=wpool_d.rearrange("g c d -> c g d")), writes=[b_wpool])
        for b_ in (b_cst, b_vrows, b_small):
            b_.lw = (s_misc, P.dma_cnt[s_misc])

        s_xm = [[P.new_dma_sem("x%d_%d" % (i, m)) for m in range(MT)] for i in range(2)]
        s_om = [P.new_dma_sem("o%d" % m) for m in range(MT)]

        def load_x_m(t, m):
            xt_ = xts[t % 2]
            P.dma("pool", s_xm[t % 2][m], lambda e, t=t, m=m, xt_=xt_: e.dma_start(out=xt_[:, m, :], in_=x_d[t * T + m * 128:t * T + (m + 1) * 128, :]),
                  writes=[b_xs[t % 2][m]])

        def load_x(t):
            for m in range(MT):
                load_x_m(t, m)
        load_x(0)

        P.op("dve", lambda e: e.tensor_copy(out=ident[:], in_=identf), reads=[b_cst], writes=[b_ident])
        P.op("dve", lambda e: e.tensor_copy(out=trib[:].rearrange("p a b -> p (a b)"), in_=cst[:, 128:384]), reads=[b_cst], writes=[b_ident])
        P.op("pe", lambda e: e.matmul(bank(0, 128), lhsT=vrows[:], rhs=identf, start=True, stop=True),
             reads=[b_vrows, b_cst], writes=pb(0))
        P.op("pe", lambda e: e.matmul(bank(1, 20), lhsT=vrows2[:], rhs=identf[0:20, 0:20], start=True, stop=True),
             reads=[b_vrows, b_cst], writes=pb(1))
        P.op("dve", lambda e: e.tensor_copy(out=vecs1[:], in_=bank(0, 128)), reads=pb(0), writes=[b_vecs])
        P.op("dve", lambda e: e.tensor_copy(out=vecs2[:], in_=bank(1, 20)), reads=pb(1), writes=[b_vecs])
        ng_fm = vecs1[:, 0:24].rearrange("p (s k) -> p s k", s=3)
        convw_fm = vecs1[:, 24:104].rearrange("p (k c) -> p k c", k=4)
        convb_fm = vecs1[:, 104:124]
        pscale_fm = vecs1[:, 124:128]
        ssmg_fm = vecs2[:, 0:12]
        c_fm = vecs2[:, 12:20]
        P.op("act", lambda e: e.activation(out=A_bc[:], in_=A_bc[:], func=AF.Exp), reads=[b_small], writes=[b_small])
        P.op("dve", lambda e: e.tensor_scalar(out=A_bc[:], in0=A_bc[:], scalar1=-1.0, scalar2=None, op0=ALU.mult),
             reads=[b_small], writes=[b_small])
        P.op("act", lambda e: e.activation(out=sc[:], in_=c_fm, func=AF.Silu), reads=[b_vecs], writes=[b_sc])
        P.op("dve", lambda e: e.tensor_copy(out=sc_b[:], in_=sc[:].unsqueeze(2).broadcast_to([128, 8, 128])),
             reads=[b_sc], writes=[b_sc])
        P.op("pool", lambda e: e.memset(state[:], 0.0), writes=b_state)
        P.op("pool", lambda e: e.memset(state_bf[:], 0.0), writes=b_statebf)
        P.op("pool", lambda e: e.memset(halo[:], 0.0), writes=[b_halo])
        P.op("pool", lambda e: e.memset(upool[:], 0.0), writes=b_up)

        wav = wada_d.rearrange("(kc p) n -> p kc n", p=128)
        for blk in range(18):
            s_i, which, half = blk // 6, (blk % 6) // 2, blk % 2
            si = ring_pos[0] % NSLOT
            ring_pos[0] += 1
            v = slot_view(si, 8, 512)
            P.dma("pool", s_wada[si], lambda e, v=v, blk=blk: e.dma_start(out=v, in_=wav[:, :, blk * 512:(blk + 1) * 512]),
                  writes=[b_ring[si]])
            bb = scr4[blk % 2]
            P.dma("sp", s_bb[blk % 2], lambda e, bb=bb, blk=blk: e.dma_start(
                out=bb[:, 0:512], in_=bada_d[blk * 512:(blk + 1) * 512].unsqueeze(0).broadcast_to([128, 512])),
                writes=[b_scr[blk % 2]])
            bk = blk % 2
            P.op("pe", [(lambda e, kc=kc, v=v, bk=bk: e.matmul(bank(bk), lhsT=sc_b[:, kc, :], rhs=v[:, kc, :],
                                                                start=(kc == 0), stop=(kc == 7))) for kc in range(8)],
                 reads=[b_sc, b_ring[si]], writes=pb(bk))
            if which == 2:
                dst = gate_bc[:, s_i, half * 512:(half + 1) * 512]
                P.op("dve", lambda e, dst=dst, bk=bk, bb=bb: e.tensor_tensor(out=dst, in0=bank(bk), in1=bb[:, 0:512], op=ALU.add),
                     reads=pb(bk) + [b_scr[blk % 2]], writes=[b_gate[s_i]] + b_sz)
                if s_i != 1:
                    P.op("dve", lambda e, dst=dst: e.tensor_scalar(out=dst, in0=dst, scalar1=0.5, scalar2=None, op0=ALU.mult),
                         reads=[b_gate[s_i]], writes=[b_gate[s_i]] + b_sz)
            else:
                tmp = scr4[2 + blk % 2]
                P.op("dve", lambda e, tmp=tmp, bk=bk, bb=bb: e.tensor_tensor(out=tmp[:, 0:512], in0=bank(bk), in1=bb[:, 0:512], op=ALU.add),
                     reads=pb(bk) + [b_scr[blk % 2]], writes=[b_scr[2 + blk % 2]])
                P.op("dve", lambda e, tmp=tmp: e.tensor_tensor(
                    out=tmp[:, 0:512].rearrange("p (a b) -> p a b", a=4), in0=tmp[:, 0:512].rearrange("p (a b) -> p a b", a=4),
                    in1=identf.unsqueeze(1).broadcast_to([128, 4, 128]), op=ALU.mult),
                    reads=[b_scr[2 + blk % 2], b_cst], writes=[b_scr[2 + blk % 2]])
                P.op("dve", lambda e, tmp=tmp, s_i=s_i, which=which, half=half: e.tensor_reduce(
                    out=modv[:, s_i, which, half * 4:(half + 1) * 4], in_=tmp[:, 0:512].rearrange("p (a b) -> p a b", a=4),
                    axis=AX.X, op=ALU.add),
                    reads=[b_scr[2 + blk % 2]], writes=[b_modv])
        P.op("dve", lambda e: e.scalar_tensor_tensor(out=gs[:], in0=modv[:, :, 1, :], scalar=1.0, in1=ng_fm, op0=ALU.add, op1=ALU.mult),
             reads=[b_modv, b_vecs], writes=[b_gs])

        is_o = lambda n: "_o" in n
        pre_order = ([n for n in order_pre if n.startswith("f0_") and not is_o(n)] + [n for n in order_pre if n.startswith("f0_o")] +
                     [n for n in order_pre if n.startswith("m_o")] + [n for n in order_pre if n.startswith("f1_o")] +
                     [n for n in order_pre if n.startswith("m_") and not is_o(n)] + [n for n in order_pre if n.startswith("f1_") and not is_o(n)])
        assert sorted(pre_order) == sorted(order_pre)
        for name in pre_order:
            u = units[name]
            sname = P.new_dma_sem("pre_" + name)
            P.dma("pool", sname, lambda e, u=u: e.dma_start(out=u["scr"], in_=u["src"]), writes=[u["buf"]])

        s_gl = [P.new_dma_sem("gl0"), P.new_dma_sem("gl1")]
        s_gs = [P.new_dma_sem("gs0"), P.new_dma_sem("gs1")]
        gate_rot = [0]
        ymT_flat = ymT[:].rearrange("p k t -> p (k t)")
        gate_store_pending = [None]

        def gate_store_flush():
            if gate_store_pending[0] is not None:
                gate_store_pending[0]()
                gate_store_pending[0] = None

        def gate_unit(name, s_i):
            gate_store_flush()
            u = units[name]
            A = u["A"]
            gi_ = gate_rot[0] % 2
            gate_rot[0] += 1
            stg = ymT_flat[:, gi_ * 4096:gi_ * 4096 + A * 1024].rearrange("p (a b) -> p a b", a=A)
            sbufs = b_ymT[gi_ * 8:gi_ * 8 + 8]
            P.dma("sp", s_gl[gi_], lambda e, stg=stg, u=u: e.dma_start(out=stg, in_=u["scr"]), reads=[u["buf"]], writes=sbufs)
            P.op("dve", lambda e, stg=stg, s_i=s_i, A=A: e.tensor_tensor(out=stg, in0=stg,
                                                                        in1=gate_bc[:, s_i, :].unsqueeze(1).broadcast_to([128, A, 1024]), op=ALU.mult),
                 reads=[b_gate[s_i]] + b_sz, writes=sbufs)

            def store(stg=stg, u=u, gi_=gi_, sbufs=sbufs):
                P.dma("sp", s_gs[gi_], lambda e, stg=stg, u=u: e.dma_start(out=u["scr"], in_=stg), reads=sbufs, writes=[u["buf"]])
            gate_store_pending[0] = store

        gate_pending = ([("f0_o%d" % g, 0) for g in range(6)] + [("m_o%d" % g, 1) for g in range(4)] + [("f1_o%d" % g, 2) for g in range(6)])

        tp_rot = [0]

        cur = {"t": 0}

        def norm_to_hT(s_i):
            xt = xts[cur["t"] % 2]; b_x = b_xs[cur["t"] % 2]
            for m in range(MT):
                P.op("act", lambda e, m=m: e.activation(out=xn[:, m, :], in_=xt[:, m, :], func=AF.Square, scale=1.0 / 32.0,
                                                         accum_out=stat[:, m:m + 1]),
                     reads=[b_x[m]], writes=(b_xn + all_yn if m == 0 else [b_xn[m]]) + [b_stat])
            P.op("act", lambda e: e.activation(out=stat[:, 4:8], in_=stat[:, 0:4], func=AF.Sqrt, bias=EPS, scale=1.0),
                 reads=[b_stat], writes=[b_stat])
            P.op("dve", lambda e: e.reciprocal(out=stat[:, 8:12], in_=stat[:, 4:8]), reads=[b_stat], writes=[b_stat])
            for m in range(MT):
                P.op("dve", lambda e, m=m: e.tensor_scalar(out=xn[:, m, :], in0=xt[:, m, :], scalar1=stat[:, 8 + m:9 + m], scalar2=None,
                                                            op0=ALU.mult),
                     reads=[b_x[m], b_stat], writes=[b_xn[m]])
            for kc in range(KC):
                hb = 8 + 2 * (tp_rot[0] % 4)
                tp_rot[0] += 1
                P.op("pe", [(lambda e, m=m, kc=kc, hb=hb: e.transpose(out=bank_bf(hb)[:, m * 128:(m + 1) * 128],
                                                                        in_=xn[:, m, kc * 128:(kc + 1) * 128], identity=ident[:]))
                            for m in range(MT)],
                     reads=b_xn + [b_ident], writes=[b_ps[hb]])
                eng = "act" if kc % 2 == 0 else "dve"
                if eng == "act":
                    P.op("act", lambda e, kc=kc, hb=hb: e.activation(out=hT[:, kc, :], in_=bank_bf(hb), func=AF.Identity,
                                                                      scale=gs[:, s_i, kc:kc + 1], bias=modv[:, s_i, 0, kc:kc + 1]),
                         reads=[b_ps[hb], b_gs, b_modv], writes=[b_hT[kc]])
                else:
                    P.op("dve", lambda e, kc=kc, hb=hb: e.tensor_scalar(out=hT[:, kc, :], in0=bank_bf(hb), scalar1=gs[:, s_i, kc:kc + 1],
                                                                         scalar2=modv[:, s_i, 0, kc:kc + 1], op0=ALU.mult, op1=ALU.add),
                         reads=[b_ps[hb], b_gs, b_modv], writes=[b_hT[kc]])

        def out_proj(unit_names, lhs, lhs_bufs, nk_total, gate_i):
            xt = xts[cur["t"] % 2]; b_x = b_xs[cur["t"] % 2]
            k = 0
            for name in unit_names:
                v, rb = load_unit(name)
                for kk in range(units[name]["A"]):
                    fns = []
                    for m in range(MT):
                        for n in range(2):
                            fns.append(lambda e, m=m, n=n, kk=kk, k=k, v=v: e.matmul(
                                bank(m * 2 + n), lhsT=lhs[:, k, m * 128:(m + 1) * 128], rhs=v[:, kk, n * 512:(n + 1) * 512],
                                start=(k == 0), stop=(k == nk_total - 1)))
                    P.op("pe", fns, reads=[rb, lhs_bufs[k]], writes=b_bank)
                    k += 1
            for m in range(MT):
                for n in range(2):
                    xs_ = xt[:, m, n * 512:(n + 1) * 512]
                    P.op("dve", lambda e, m=m, n=n, xs_=xs_: e.tensor_tensor(out=xs_, in0=bank(m * 2 + n), in1=xs_, op=ALU.add),
                         reads=pb(m * 2 + n) + [b_x[m]], writes=[b_x[m]])

        def ffn(f, gate_i):
            va = vb = None
            for j in range(NJ):
                g, jo = j // 4, j % 4
                if jo == 0:
                    va, ra = load_unit("f%d_a%d" % (f, g))
                    vb, rb = load_unit("f%d_b%d" % (f, g))
                ba, bb_ = 2 * (j % 2), 2 * (j % 2) + 1
                P.op("pe", [(lambda e, kc=kc, va=va, jo=jo, ba=ba: e.matmul(bank(ba), lhsT=va[:, kc, jo * 128:(jo + 1) * 128], rhs=hT[:, kc, :],
                                                                           start=(kc == 0), stop=(kc == 7))) for kc in range(8)],
                     reads=[ra] + b_hT, writes=pb(ba))
                P.op("pe", [(lambda e, kc=kc, vb=vb, jo=jo, bb_=bb_: e.matmul(bank(bb_), lhsT=vb[:, kc, jo * 128:(jo + 1) * 128], rhs=hT[:, kc, :],
                                                                             start=(kc == 0), stop=(kc == 7))) for kc in range(8)],
                     reads=[rb] + b_hT, writes=pb(bb_))
                if gate_pending and j >= 6:
                    gate_unit(*gate_pending.pop(0))
                elif j >= 6:
                    gate_store_flush()
                sa = scr4[2 + j % 2]
                P.op("act", lambda e, sa=sa, ba=ba: e.activation(out=sa[:, 0:512], in_=bank(ba), func=AF.Silu),
                     reads=pb(ba), writes=[b_scr[2 + j % 2]])
                P.op("dve", lambda e, sa=sa, bb_=bb_, j=j: e.tensor_tensor(out=gT[:, j, :], in0=sa[:, 0:512], in1=bank(bb_), op=ALU.mult),
                     reads=pb(bb_) + [b_scr[2 + j % 2]], writes=[b_gT[j]])
            gate_store_flush()
            assert not gate_pending
            out_proj(["f%d_o%d" % (f, g) for g in range(6)], gT, b_gT, NJ, gate_i)

        rot = {"fm": 0}

        def mixer(t):
            v, rb = load_unit("m_up")
            for gi in range(4):
                bk = rot["fm"] % 4; rot["fm"] += 1
                P.op("pe", [(lambda e, kc=kc, v=v, gi=gi, bk=bk: e.matmul(bank(bk), lhsT=v[:, kc, gi * 128:(gi + 1) * 128], rhs=hT[:, kc, :],
                                                                         start=(kc == 0), stop=(kc == 7))) for kc in range(8)],
                     reads=[rb] + b_hT, writes=pb(bk))
                P.op("act", lambda e, gi=gi, bk=bk: e.copy(out=upool[:, gi, 16:528], in_=bank(bk)), reads=pb(bk), writes=[b_up[gi]])
            conv_pending = [None]

            def conv_unit(name, cc0):
                v, rb = load_unit(name)
                for q in range(4):
                    cc = cc0 + q
                    bk = rot["fm"] % 4; rot["fm"] += 1
                    P.op("pe", [(lambda e, kc=kc, v=v, q=q, bk=bk: e.matmul(bank(bk), lhsT=v[:, kc, q * 128:(q + 1) * 128], rhs=hT[:, kc, :],
                                                                           start=(kc == 0), stop=(kc == 7))) for kc in range(8)],
                         reads=[rb] + b_hT, writes=pb(bk))
                    raw = scr4[cc % 2]; acc = scr4[2 + cc % 2]
                    braw = b_scr[cc % 2]; bacc = b_scr[2 + cc % 2]
                    P.op("act", lambda e, raw=raw, bk=bk: e.copy(out=raw[:, 3:515], in_=bank(bk)), reads=pb(bk), writes=[braw])
                    P.op("act", lambda e, acc=acc, bk=bk, cc=cc: e.activation(out=acc[:, 0:512], in_=bank(bk), func=AF.Identity,
                                                                              scale=convw_fm[:, 3, cc:cc + 1], bias=convb_fm[:, cc:cc + 1]),
                         reads=pb(bk) + [b_vecs], writes=[bacc])
                    P.op("act", lambda e, raw=raw, cc=cc: e.copy(out=raw[:, 0:3], in_=halo[:, cc, :]), reads=[b_halo], writes=[braw])
                    if conv_pending[0] is not None:
                        conv_pending[0]()
                    for k in range(3):
                        P.op("dve", lambda e, raw=raw, acc=acc, cc=cc, k=k: e.scalar_tensor_tensor(
                            out=acc[:, 0:512], in0=raw[:, k:k + 512], scalar=convw_fm[:, k, cc:cc + 1], in1=acc[:, 0:512],
                            op0=ALU.mult, op1=ALU.add), reads=[braw, bacc, b_vecs], writes=[bacc])
                    P.op("pool", lambda e, raw=raw, cc=cc: e.tensor_copy(out=halo[:, cc, :], in_=raw[:, 512:515]), reads=[braw], writes=[b_halo])

                    def fin(acc=acc, cc=cc, bacc=bacc):
                        P.op("act", lambda e, acc=acc, cc=cc: e.activation(out=gT[:, cc, :], in_=acc[:, 0:512], func=AF.Silu),
                             reads=[bacc], writes=[b_gT[cc]])
                    conv_pending[0] = fin

            def conv_flush():
                if conv_pending[0] is not None:
                    conv_pending[0]()
                    conv_pending[0] = None
            conv_unit("m_B", 12)
            conv_unit("m_C", 16)
            v, rb = load_unit("m_dt")
            pdt = bank(6, 96)
            P.op("pe", [(lambda e, kc=kc, m=m, v=v: e.matmul(bank(6)[:, m * NH:(m + 1) * NH], lhsT=hT[:, kc, m * 128:(m + 1) * 128], rhs=v[:, kc, :],
                                                              start=(kc == 0), stop=(kc == 7))) for m in range(MT) for kc in range(8)],
                 reads=[rb] + b_hT, writes=pb(6))
            dtr, dtv, av, csv, dec_out, cdec, dst_, dts = [dtc[:, i, :] for i in range(8)]
            m24 = lambda ap: ap.rearrange("p (m h) -> p m h", m=MT)
            P.op("dve", lambda e: e.tensor_tensor(out=m24(dtr), in0=m24(pdt), in1=dtb_bc[:].unsqueeze(1).broadcast_to([128, MT, NH]), op=ALU.add),
                 reads=pb(6) + [b_small], writes=[b_dtc])
            P.op("act", lambda e: e.activation(out=dtr, in_=dtr, func=AF.Exp), reads=[b_dtc], writes=[b_dtc])
            P.op("act", lambda e: e.activation(out=dtv, in_=dtr, func=AF.Ln, bias=1.0), reads=[b_dtc], writes=[b_dtc])
            P.op("dve", lambda e: e.tensor_tensor(out=m24(av), in0=m24(dtv), in1=A_bc[:].unsqueeze(1).broadcast_to([128, MT, NH]), op=ALU.mult),
                 reads=[b_dtc, b_small], writes=[b_dtc])
            P.op("dve", lambda e: e.tensor_copy(out=avs[:, 0, :], in_=av), reads=[b_dtc], writes=[b_avs])
            P.op("dve", lambda e: e.tensor_tensor(out=avs[:, 1, :], in0=av, in1=avs[:, 0, :], op=ALU.subtract), reads=[b_dtc, b_avs], writes=[b_avs])
            P.op("pe", lambda e: e.matmul(bank(6)[:, 128:224], lhsT=tri, rhs=av, start=True, stop=True), reads=[b_dtc, b_cst], writes=pb(6))
            P.op("pe", lambda e: e.matmul(bank(6)[:, 256:352], lhsT=ones, rhs=av, start=True, stop=True), reads=[b_dtc, b_cst], writes=pb(6))
            P.op("act", lambda e: e.copy(out=csv, in_=bank(6)[:, 128:224]), reads=pb(6), writes=[b_dtc])
            P.op("act", lambda e: e.activation(out=dec_out, in_=bank(6)[:, 128:224], func=AF.Exp), reads=pb(6), writes=[b_dtc])
            P.op("act", lambda e: e.activation(out=cdec, in_=bank(6)[:, 256:352], func=AF.Exp), reads=pb(6), writes=[b_dtc])
            P.op("dve", lambda e: e.tensor_tensor(out=dst_, in0=bank(6)[:, 256:352], in1=csv, op=ALU.subtract), reads=pb(6) + [b_dtc], writes=[b_dtc])
            P.op("act", lambda e: e.activation(out=dst_, in_=dst_, func=AF.Exp), reads=[b_dtc], writes=[b_dtc])
            P.op("dve", lambda e: e.tensor_tensor(out=dts, in0=dst_, in1=dtv, op=ALU.mult), reads=[b_dtc], writes=[b_dtc])
            for i in range(3):
                conv_unit("m_xs%d" % i, i * 4)
            conv_flush()
            for i in range(3):
                v, rb = load_unit("m_z%d" % i)
                for m in range(MT):
                    bk = rot["fm"] % 4; rot["fm"] += 1
                    P.op("pe", [(lambda e, kc=kc, v=v, m=m, bk=bk: e.matmul(bank(bk), lhsT=hT[:, kc, m * 128:(m + 1) * 128], rhs=v[:, kc, :],
                                                                           start=(kc == 0), stop=(kc == 7))) for kc in range(8)],
                         reads=[rb] + b_hT, writes=pb(bk))
                    P.op("act", lambda e, m=m, i=i, bk=bk: e.activation(out=sz[:, m, i * 512:(i + 1) * 512], in_=bank(bk), func=AF.Silu),
                         reads=pb(bk), writes=[b_sz[m]])
            for gi in range(4):
                cur = upool[:, gi, :]
                curb = b_up[gi]
                w = 1
                for lvl in range(gi + 1):
                    dst = scr4[lvl % 2]; dstb = b_scr[lvl % 2]
                    lo = 2 * w - 1
                    P.op("pool", lambda e, cur=cur, dst=dst, w=w, lo=lo: e.tensor_tensor(out=dst[:, lo:528], in0=cur[:, lo:528],
                                                                                       in1=cur[:, lo - w:528 - w], op=ALU.add),
                         reads=[curb], writes=[dstb])
                    cur = dst[:]; curb = dstb; w *= 2
                P.op("dve", lambda e, cur=cur, gi=gi, w=w: e.scalar_tensor_tensor(out=pooledT[:, gi, :], in0=cur[:, 16:528], scalar=1.0 / w,
                                                                                 in1=upool[:, gi, 16:528], op0=ALU.mult, op1=ALU.subtract),
                     reads=[curb, b_up[gi]], writes=[b_pooled[gi]])
                if t == 0:
                    tmpc = scr4[2]
                    P.op("dve", lambda e, cur=cur, gi=gi, tmpc=tmpc: e.tensor_tensor(out=tmpc[:, 0:16], in0=cur[:, 16:32],
                                                                                    in1=invcnt[:, gi * 16:(gi + 1) * 16], op=ALU.mult),
                         reads=[curb, b_cst], writes=[b_scr[2]])
                    P.op("dve", lambda e, gi=gi, tmpc=tmpc: e.tensor_tensor(out=pooledT[:, gi, 0:16], in0=tmpc[:, 0:16],
                                                                           in1=upool[:, gi, 16:32], op=ALU.subtract),
                         reads=[b_scr[2], b_up[gi]], writes=[b_pooled[gi]])
                P.op("pool", lambda e, gi=gi: e.tensor_copy(out=upool[:, gi, 0:16], in_=upool[:, gi, 512:528]), reads=[b_up[gi]], writes=[b_up[gi]])
                bk = rot["fm"] % 4; rot["fm"] += 1
                P.op("pe", lambda e, gi=gi, bk=bk: e.matmul(bank(bk), lhsT=wpool[:, gi, :], rhs=pooledT[:, gi, :], start=True, stop=True),
                     reads=[b_wpool, b_pooled[gi]], writes=pb(bk))
                P.op("act", lambda e, gi=gi, bk=bk: e.activation(out=ymT[:, gi, :], in_=bank(bk), func=AF.Identity, scale=pscale_fm[:, gi:gi + 1]),
                     reads=pb(bk) + [b_vecs], writes=[b_ymT[gi]])
            xsT = lambda cc, m: gT[:, cc, m * 128:(m + 1) * 128]
            BTm = lambda g, m: gT[:, 12 + g, m * 128:(m + 1) * 128]
            CTm = lambda g, m: gT[:, 16 + g, m * 128:(m + 1) * 128]
            h24 = lambda ap, m: ap.rearrange("p (m h) -> p m h", m=MT)[:, m, :]
            psx = PS[:, 4 * 512:4 * 512 + 768].bitcast(BF16)
            psb = bank_bf(11)
            def prep_s(m):
                i2 = m % 2
                P.op("pe", [(lambda e, g=g, m=m: e.matmul(bank(6)[:, g * 128:(g + 1) * 128], lhsT=BTm(g, m), rhs=CTm(g, m), start=True, stop=True))
                            for g in range(4)], reads=b_gT[12:20], writes=pb(6))
                P.op("dve", lambda e, i2=i2: e.tensor_tensor(out=smk[i2][:], in0=bank(6).rearrange("p (g l) -> p g l", g=4),
                                                             in1=tri.unsqueeze(1).broadcast_to([128, 4, 128]), op=ALU.mult),
                     reads=pb(6) + [b_cst], writes=[b_smk[i2]])

            bc6 = lambda ap: ap.unsqueeze(2).broadcast_to([128, 6, 64])

            def a1(m, g, ig):
                hs = slice(g * 6, (g + 1) * 6)
                for part in range(2):
                    P.op("dve", lambda e, ig=ig, m=m, hs=hs, part=part: e.tensor_tensor(
                        out=rhsD[ig][:, part, :].rearrange("p (r l) -> p r l", r=6),
                        in0=h24(avs[:, part, :], m)[:, hs].unsqueeze(2).broadcast_to([128, 6, 128]),
                        in1=trib[:, 0, :].unsqueeze(1).broadcast_to([128, 6, 128]), op=ALU.mult),
                        reads=[b_avs, b_ident], writes=[b_rhsD[ig]])
                P.op("pe", [lambda e, ig=ig: e.matmul(bank(0), lhsT=trib[:, 1, :], rhs=rhsD[ig][:, 0, 0:512], start=True, stop=False),
                            lambda e, ig=ig: e.matmul(bank(0), lhsT=trib[:, 1, :], rhs=rhsD[ig][:, 1, 0:512], start=False, stop=True),
                            lambda e, ig=ig: e.matmul(bank(1, 256), lhsT=trib[:, 1, :], rhs=rhsD[ig][:, 0, 512:768], start=True, stop=False),
                            lambda e, ig=ig: e.matmul(bank(1, 256), lhsT=trib[:, 1, :], rhs=rhsD[ig][:, 1, 512:768], start=False, stop=True)],
                     reads=[b_rhsD[ig], b_ident], writes=pb(0) + pb(1))
                P.op("act", lambda e, ig=ig: e.activation(out=expD[ig][:].rearrange("p r l -> p (r l)"), in_=PS[:, 0:768], func=AF.Exp),
                     reads=pb(0) + pb(1), writes=[b_expD[ig]])

            def a2_pe(m, g, ig):
                bk = 4 + ig
                pst = bank(bk).bitcast(BF16)
                P.op("pe", [(lambda e, q=q, m=m, g=g, pst=pst: e.transpose(out=pst[:, q * 128:(q + 1) * 128], in_=xsT(3 * g + q, m), identity=ident[:]))
                            for q in range(3)] +
                           [lambda e, m=m, g=g, pst=pst: e.transpose(out=pst[:, 384:512], in_=BTm(g, m), identity=ident[:])],
                     reads=b_gT[3 * g:3 * g + 3] + [b_gT[12 + g], b_ident], writes=pb(bk))

            def a2_rest(m, g, ig):
                i2 = m % 2
                hs = slice(g * 6, (g + 1) * 6)
                bk = 4 + ig
                pst = bank(bk).bitcast(BF16)
                px = pst[:, 0:384].rearrange("p (h d) -> p h d", h=6)
                P.op("dve", lambda e, m=m, hs=hs, ig=ig, px=px: e.tensor_tensor(out=xd[ig][:], in0=px, in1=bc6(h24(dtv, m)[:, hs]), op=ALU.mult),
                     reads=pb(bk) + [b_dtc], writes=[b_xd[ig]])
                P.op("dve", lambda e, m=m, hs=hs, ig=ig, px=px: e.tensor_tensor(out=xdd[ig][:], in0=px, in1=bc6(h24(dts, m)[:, hs]), op=ALU.mult),
                     reads=pb(bk) + [b_dtc], writes=[b_xdd[ig]])
                P.op("dve", lambda e, hs=hs, ig=ig, px=px: e.tensor_tensor(out=xsD[ig][:], in0=px, in1=bc6(dsk_bc[:, hs]), op=ALU.mult),
                     reads=pb(bk) + [b_small], writes=[b_xsD[ig]])
                P.op("act", lambda e, ig=ig, pst=pst: e.copy(out=btok[ig][:], in_=pst[:, 384:512]), reads=pb(bk), writes=[b_btok[ig]])
                P.op("pool", lambda e, ig=ig, i2=i2, g=g: e.tensor_tensor(
                    out=MTt[ig][:], in0=expD[ig][:], in1=smk[i2][:, g, :].unsqueeze(1).broadcast_to([128, 6, 128]), op=ALU.mult),
                    reads=[b_expD[ig], b_smk[i2]], writes=[b_MT[ig]])

            def state_decay(m, g):
                hs = slice(g * 6, (g + 1) * 6)
                stg = state[:, hs, :]
                P.op("pool", lambda e, stg=stg, m=m, hs=hs: e.tensor_tensor(
                    out=stg, in0=stg, in1=h24(cdec, m)[:, hs].unsqueeze(2).broadcast_to([128, 6, 64]), op=ALU.mult),
                    reads=[b_state[g], b_dtc], writes=[b_state[g]])

            def b_pe(m, g, ig):
                hs = slice(g * 6, (g + 1) * 6)
                P.op("pe", [(lambda e, r=r, ig=ig: e.matmul(bank(2)[:, r * 64:(r + 1) * 64], lhsT=MTt[ig][:, r, :], rhs=xd[ig][:, r, :],
                                                            start=True, stop=True)) for r in range(6)],
                     reads=[b_MT[ig], b_xd[ig]], writes=pb(2))
                sbf = state_bf[:, hs, :].rearrange("p h d -> p (h d)")
                P.op("pe", lambda e, g=g, m=m, sbf=sbf: e.matmul(bank(3, 384), lhsT=CTm(g, m), rhs=sbf, start=True, stop=True),
                     reads=b_gT[16:20] + [b_statebf[g]], writes=pb(3))
                xddg = xdd[ig][:].rearrange("p h d -> p (h d)")
                P.op("pe", lambda e, ig=ig, xddg=xddg: e.matmul(bank(7, 384), lhsT=btok[ig][:], rhs=xddg, start=True, stop=True),
                     reads=[b_btok[ig], b_xdd[ig]], writes=pb(7))

            def b_dve(m, g, ig):
                hs = slice(g * 6, (g + 1) * 6)
                stg = state[:, hs, :]
                ytg = yt[:, hs, :]
                P.op("dve", lambda e, ytg=ytg, ig=ig: e.tensor_tensor(out=ytg, in0=bank(2, 384).rearrange("p (h d) -> p h d", h=6),
                                                                     in1=xsD[ig][:], op=ALU.add),
                     reads=pb(2) + [b_xsD[ig]], writes=[b_yt[g]])
                P.op("dve", lambda e, m=m, hs=hs: e.tensor_tensor(out=t1[:], in0=bank(3, 384).rearrange("p (h d) -> p h d", h=6),
                                                                  in1=h24(dec_out, m)[:, hs].unsqueeze(2).broadcast_to([128, 6, 64]), op=ALU.mult),
                     reads=pb(3) + [b_dtc], writes=[b_t1])
                P.op("dve", lambda e, stg=stg: e.tensor_tensor(out=stg, in0=stg, in1=bank(7, 384).rearrange("p (h d) -> p h d", h=6), op=ALU.add),
                     reads=pb(7) + [b_state[g]], writes=[b_state[g]])
                P.op("act", lambda e, stg=stg, hs=hs: e.copy(out=state_bf[:, hs, :], in_=stg), reads=[b_state[g]], writes=[b_statebf[g]])

            def b_tail(m, g, ig):
                hs = slice(g * 6, (g + 1) * 6)
                ytg = yt[:, hs, :]
                ytgf = ytg.rearrange("p h d -> p (h d)")
                P.op("pool", lambda e, ytg=ytg: e.tensor_tensor(out=ytg, in0=ytg, in1=t1[:], op=ALU.add),
                     reads=[b_t1, b_yt[g]], writes=[b_yt[g]])
                P.op("pool", lambda e, ytgf=ytgf, m=m, g=g: e.tensor_tensor(out=ytgf, in0=ytgf, in1=sz[:, m, g * 384:(g + 1) * 384], op=ALU.mult),
                     reads=[b_yt[g], b_sz[m]], writes=[b_yt[g]])
                sq = stat2[:, ig, 0:1]; ln_ = stat2[:, ig, 1:2]; rs = stat2[:, ig, 2:3]
                P.op("act", lambda e, ytgf=ytgf, m=m, g=g, sq=sq: e.activation(out=yn[:, m, g * 384:(g + 1) * 384], in_=ytgf, func=AF.Square,
                                                                             scale=float(1.0 / np.sqrt(384.0)), accum_out=sq),
                     reads=[b_yt[g]], writes=[b_yn[m][g], b_stat2[ig]] + (b_xn if (m == 0 and g == 0) else []))
                P.op("act", lambda e, sq=sq, ln_=ln_: e.activation(out=ln_, in_=sq, func=AF.Ln, bias=EPS, scale=1.0),
                     reads=[b_stat2[ig]], writes=[b_stat2[ig]])
                P.op("act", lambda e, ln_=ln_, rs=rs: e.activation(out=rs, in_=ln_, func=AF.Exp, scale=-0.5),
                     reads=[b_stat2[ig]], writes=[b_stat2[ig]])
                P.op("act", lambda e, ytgf=ytgf, m=m, g=g, rs=rs: e.activation(out=yn[:, m, g * 384:(g + 1) * 384], in_=ytgf, func=AF.Identity, scale=rs),
                     reads=[b_yt[g], b_stat2[ig]], writes=[b_yn[m][g]])

            items = [(m, g) for m in range(MT) for g in range(4)]
            NI = len(items)
            done_s = set()

            def emit_a1(i):
                m, g = items[i]
                if m not in done_s:
                    prep_s(m)
                    done_s.add(m)
                a1(m, g, i % 2)
            for i in range(2):
                emit_a1(i)
                a2_pe(*items[i], i % 2)
                a2_rest(*items[i], i % 2)
            state_decay(*items[0])
            for i, (m, g) in enumerate(items):
                ig = i % 2
                if i + 2 < NI:
                    emit_a1(i + 2)
                b_pe(m, g, ig)
                if i + 2 < NI:
                    a2_pe(*items[i + 2], ig)
                b_dve(m, g, ig)
                if i + 1 < NI:
                    state_decay(*items[i + 1])
                b_tail(m, g, ig)
                if i + 2 < NI:
                    a2_rest(*items[i + 2], ig)
            for cc in range(12):
                hb = 8 + 2 * (tp_rot[0] % 4)
                tp_rot[0] += 1
                P.op("pe", [(lambda e, m=m, cc=cc, hb=hb: e.transpose(out=bank_bf(hb)[:, m * 128:(m + 1) * 128],
                                                                        in_=yn[:, m, cc * 128:(cc + 1) * 128], identity=ident[:]))
                            for m in range(MT)], reads=[b_yn[m][cc // 3] for m in range(MT)] + [b_ident], writes=[b_ps[hb]])
                if cc % 2 == 0:
                    P.op("act", lambda e, cc=cc, hb=hb: e.activation(out=ymT[:, 4 + cc, :], in_=bank_bf(hb), func=AF.Identity,
                                                                      scale=ssmg_fm[:, cc:cc + 1]),
                         reads=[b_ps[hb], b_vecs], writes=[b_ymT[4 + cc]])
                else:
                    P.op("dve", lambda e, cc=cc, hb=hb: e.tensor_scalar(out=ymT[:, 4 + cc, :], in0=bank_bf(hb), scalar1=ssmg_fm[:, cc:cc + 1],
                                                                         scalar2=None, op0=ALU.mult),
                         reads=[b_ps[hb], b_vecs], writes=[b_ymT[4 + cc]])
            out_proj(["m_o%d" % g for g in range(4)], ymT, b_ymT, 16, 1)

        def final_and_store(t, do_norm):
            xt = xts[t % 2]; b_x = b_xs[t % 2]
            if do_norm:
                for m in range(MT):
                    P.op("act", lambda e, m=m: e.activation(out=xn[:, m, :], in_=xt[:, m, :], func=AF.Square, scale=1.0 / 32.0,
                                                             accum_out=stat[:, m:m + 1]),
                         reads=[b_x[m]], writes=(b_xn + all_yn if m == 0 else [b_xn[m]]) + [b_stat])
                P.op("act", lambda e: e.activation(out=stat[:, 4:8], in_=stat[:, 0:4], func=AF.Sqrt, bias=EPS, scale=1.0),
                     reads=[b_stat], writes=[b_stat])
                P.op("dve", lambda e: e.reciprocal(out=stat[:, 8:12], in_=stat[:, 4:8]), reads=[b_stat], writes=[b_stat])
                ostage = gT[:].rearrange("p j t -> p (j t)")[:, 0:8192].bitcast(F32).rearrange("p (m d) -> p m d", m=MT)
                toks = []
                for m in range(MT):
                    P.op("dve", lambda e, m=m: e.scalar_tensor_tensor(out=ostage[:, m, :], in0=xt[:, m, :], scalar=stat[:, 8 + m:9 + m],
                                                                     in1=fg_bc[:], op0=ALU.mult, op1=ALU.mult),
                         reads=[b_x[m], b_stat, b_small], writes=b_gT[4 * m:4 * m + 4])
                    toks.append(P.dma("sp", s_om[m], lambda e, t=t, m=m: e.dma_start(out=out_d[t * T + m * 128:t * T + (m + 1) * 128, :],
                                                                                   in_=ostage[:, m, :]),
                                      reads=b_gT[4 * m:4 * m + 4], writes=[b_outm[m]]))
                return toks
            toks = []
            for m in range(MT):
                toks.append(P.dma("sp", s_om[m], lambda e, t=t, m=m: e.dma_start(out=out_d[t * T + m * 128:t * T + (m + 1) * 128, :], in_=xt[:, m, :]),
                                  reads=[b_x[m]], writes=[b_outm[m]]))
            return toks

        last = None
        for t in range(ntiles):
            cur["t"] = t
            if t + 1 < ntiles:
                load_x(t + 1)
            if stage >= 1:
                norm_to_hT(0)
                if stage != 11:
                    ffn(0, 0)
            if stage >= 2:
                norm_to_hT(1)
                mixer(t)
            if stage >= 3:
                norm_to_hT(2)
                ffn(1, 2)
            last = final_and_store(t, stage >= 3)
        P.wait_all("sp", last)
        P.emit()
    return nc


def make_consts():
    cst = np.zeros((128, 576), np.float32)
    cst[:, 0:128] = np.eye(128, dtype=np.float32)
    k = np.arange(128)
    cst[:, 128:256] = (k[:, None] <= k[None, :]).astype(np.float32)
    cst[:, 256:384] = (k[:, None] > k[None, :]).astype(np.float32)
    cst[:, 384:512] = 1.0
    tt = np.arange(16)
    for gi, w in enumerate((2, 4, 8, 16)):
        cst[:, 512 + gi * 16:512 + (gi + 1) * 16] = (1.0 / np.minimum(tt + 1, w)).astype(np.float32)[None, :]
    return cst


_CACHE = {}


def make_in_maps(inputs, ncores=8):
    f = lambda a: np.ascontiguousarray(np.asarray(a, dtype=np.float32))
    shared = {
        "w_ada": f(inputs["w_ada"][0]), "b_ada": f(inputs["b_ada"][0]), "norm_g": f(inputs["norm_g"][0]),
        "ffn1_in": f(inputs["ffn1_in"][0]), "ffn1_out": f(inputs["ffn1_out"][0]), "w_in": f(inputs["w_in"][0]),
        "w_pool": f(inputs["w_pool"][0]), "pool_scale": f(inputs["pool_scale"][0]), "conv_w": f(inputs["conv_w"][0]),
        "conv_b": f(inputs["conv_b"][0]), "dt_bias": f(inputs["dt_bias"][0]), "a_log": f(inputs["a_log"][0]),
        "d_skip": f(inputs["d_skip"][0]), "ssm_norm_g": f(inputs["ssm_norm_g"][0]), "w_out": f(inputs["w_out"][0]),
        "ffn2_in": f(inputs["ffn2_in"][0]), "ffn2_out": f(inputs["ffn2_out"][0]), "final_g": f(inputs["final_g"]),
        "cst": make_consts(),
    }
    x = np.asarray(inputs["x"], dtype=np.float32)
    c = np.asarray(inputs["c"], dtype=np.float32)
    maps = []
    for b in range(ncores):
        m = dict(shared)
        m["x"] = np.ascontiguousarray(x[b])
        m["c"] = np.ascontiguousarray(c[b])
        maps.append(m)
    return maps


def kernel(**inputs):
    if "nc" not in _CACHE:
        _CACHE["nc"] = build_program()
    nc = _CACHE["nc"]
    maps = make_in_maps(inputs, 8)
    res = run_bass_kernel_spmd(nc, maps, core_ids=list(range(8)))
    return np.stack([np.asarray(r["out"], dtype=np.float32) for r in res.results], axis=0)
```

```python
from contextlib import ExitStack
import numpy as np
import concourse.bass as bass
import concourse.mybir as mybir
from concourse.bass_utils import run_bass_kernel_spmd

F32 = mybir.dt.float32
BF16 = mybir.dt.bfloat16
AF = mybir.ActivationFunctionType
ALU = mybir.AluOpType
AX = mybir.AxisListType

D = 1024
KC = 8
S = 4096
T = 512
MT = 4
DFF = 2816
NJ = 22
NH = 24
EPS = 1e-5
NSLOT = 3
SLOT_ELEMS = 4096


class Buf:
    __slots__ = ("name", "lw", "rd", "excl")

    def __init__(self, name, excl=False):
        self.name = name
        self.lw = None
        self.rd = {}
        self.excl = excl


class Prog:
    ENG = ("pe", "act", "dve", "pool", "sp")

    def __init__(self, nc, stack):
        self.nc = nc
        self.stack = stack
        self.streams = {e: [] for e in self.ENG}
        self.sem = {e: stack.enter_context(nc.semaphore("prog_" + e)) for e in self.ENG}
        self.cnt = {e: 0 for e in self.ENG}
        self.waited = {e: {} for e in self.ENG}
        self.dma_sems = {}
        self.dma_cnt = {}

    def new_dma_sem(self, name):
        self.dma_sems[name] = self.stack.enter_context(self.nc.semaphore("dma_" + name))
        self.dma_cnt[name] = 0
        return name

    def _semh(self, key):
        return self.sem[key] if key in self.sem else self.dma_sems[key]

    def _collect(self, eng, reads, writes):
        need = {}

        def add(tok):
            if tok is None:
                return
            k, v = tok
            if k == eng and eng == "pe":
                return
            if need.get(k, 0) < v:
                need[k] = v
        for b in reads:
            add(b.lw)
        for b in writes:
            add(b.lw)
            for k, v in b.rd.items():
                add((k, v))
        out = []
        w = self.waited[eng]
        for k, v in need.items():
            if w.get(k, 0) < v:
                w[k] = v
                out.append((k, v))
        return out

    def op(self, eng, fns, reads=(), writes=()):
        if callable(fns):
            fns = [fns]
        writes = list(writes) + [b for b in reads if b.excl]
        reads = [b for b in reads if not b.excl]
        waits = self._collect(eng, reads, writes)
        self.cnt[eng] += 1
        idx = self.cnt[eng]
        self.streams[eng].append((waits, fns, (eng, 1)))
        for b in reads:
            if b.rd.get(eng, 0) < idx:
                b.rd[eng] = idx
        for b in writes:
            b.lw = (eng, idx)
            b.rd = {}
        return (eng, idx)

    def dma(self, qeng, semname, fn, reads=(), writes=()):
        waits = self._collect(qeng, reads, writes)
        prev = self.dma_cnt[semname]
        if prev > 0 and self.waited[qeng].get(semname, 0) < prev:
            self.waited[qeng][semname] = prev
            waits.append((semname, prev))
        self.dma_cnt[semname] += 16
        val = self.dma_cnt[semname]
        self.streams[qeng].append((waits, [fn], (semname, 16)))
        for b in reads:
            if b.rd.get(semname, 0) < val:
                b.rd[semname] = val
        for b in writes:
            b.lw = (semname, val)
            b.rd = {}
        return (semname, val)

    def wait_all(self, eng, toks):
        waits = []
        for k, v in toks:
            if self.waited[eng].get(k, 0) < v:
                self.waited[eng][k] = v
                waits.append((k, v))
        self.streams[eng].append((waits, [], None))

    def emit(self):
        engmap = {"pe": "tensor", "act": "scalar", "dve": "vector", "pool": "gpsimd", "sp": "sync"}
        with self.nc.Block() as block:
            for e in self.ENG:
                stream = self.streams[e]

                def body(engine, stream=stream):
                    for waits, fns, inc in stream:
                        for k, v in waits:
                            engine.wait_ge(self._semh(k), v)
                        n = len(fns)
                        for i, f in enumerate(fns):
                            ins = f(engine)
                            if i == n - 1 and inc is not None:
                                ins.then_inc(self._semh(inc[0]), inc[1])
                getattr(block, engmap[e])(body)


def build_program(ntiles=8, stage=3):
    nc = bass.Bass("TRN2", target_bir_lowering=False)
    dram_in = lambda name, shape: nc.dram_tensor(name, shape, F32, kind="ExternalInput").ap()
    x_d = dram_in("x", [S, D])
    c_d = dram_in("c", [D])
    wada_d = dram_in("w_ada", [D, 9 * D])
    bada_d = dram_in("b_ada", [9 * D])
    ng_d = dram_in("norm_g", [3, D])
    ffn_in_d = [dram_in("ffn1_in", [D, 2 * DFF]), dram_in("ffn2_in", [D, 2 * DFF])]
    ffn_out_d = [dram_in("ffn1_out", [DFF, D]), dram_in("ffn2_out", [DFF, D])]
    win_d = dram_in("w_in", [D, 4632])
    wpool_d = dram_in("w_pool", [4, 128, 128])
    pscale_d = dram_in("pool_scale", [512])
    convw_d = dram_in("conv_w", [4, 2560])
    convb_d = dram_in("conv_b", [2560])
    dtb_d = dram_in("dt_bias", [NH])
    alog_d = dram_in("a_log", [NH])
    dskip_d = dram_in("d_skip", [NH])
    ssmg_d = dram_in("ssm_norm_g", [1536])
    wout_d = dram_in("w_out", [2048, D])
    fg_d = dram_in("final_g", [D])
    cst_d = dram_in("cst", [128, 576])
    out_d = nc.dram_tensor("out", [S, D], F32, kind="ExternalOutput").ap()

    with ExitStack() as st:
        P = Prog(nc, st)
        sb = lambda name, shape, dt: st.enter_context(nc.sbuf_tensor("sb_" + name, shape, dt))

        units = {}
        order_pre = []

        def add_unit(name, src, A, Bc):
            scr = nc.dram_tensor("scr_" + name, [128, A, Bc], BF16).ap()
            units[name] = dict(scr=scr, src=src, A=A, Bc=Bc, buf=Buf("scr_" + name))
            order_pre.append(name)

        def add_ffn_units(f):
            wv = ffn_in_d[f].rearrange("(kc p) n -> p kc n", p=128)
            for g in range(6):
                nc_ = 512 if g < 5 else 256
                add_unit("f%d_a%d" % (f, g), wv[:, :, g * 512:g * 512 + nc_], 8, nc_)
                add_unit("f%d_b%d" % (f, g), wv[:, :, DFF + g * 512:DFF + g * 512 + nc_], 8, nc_)
            wo = ffn_out_d[f].rearrange("(k p) n -> p k n", p=128)
            for g in range(6):
                nk = 4 if g < 5 else 2
                add_unit("f%d_o%d" % (f, g), wo[:, g * 4:g * 4 + nk, :], nk, 1024)

        add_ffn_units(0)
        wiv = win_d.rearrange("(kc p) n -> p kc n", p=128)
        add_unit("m_up", wiv[:, :, 0:512], 8, 512)
        add_unit("m_B", wiv[:, :, 3584:4096], 8, 512)
        add_unit("m_C", wiv[:, :, 4096:4608], 8, 512)
        add_unit("m_dt", wiv[:, :, 4608:4632], 8, 24)
        for i in range(3):
            add_unit("m_xs%d" % i, wiv[:, :, 2048 + i * 512:2048 + (i + 1) * 512], 8, 512)
        for i in range(3):
            add_unit("m_z%d" % i, wiv[:, :, 512 + i * 512:512 + (i + 1) * 512], 8, 512)
        wov = wout_d.rearrange("(k p) n -> p k n", p=128)
        for g in range(4):
            add_unit("m_o%d" % g, wov[:, g * 4:(g + 1) * 4, :], 4, 1024)
        add_ffn_units(1)

        cst = sb("cst", [128, 576], F32)
        identf = cst[:, 0:128]
        tri = cst[:, 128:256]
        ustr = cst[:, 256:384]
        ones = cst[:, 384:512]
        invcnt = cst[:, 512:576]
        ident = sb("ident", [128, 128], BF16)
        vecs1 = sb("vecs1", [128, 128], F32)
        vecs2 = sb("vecs2", [128, 20], F32)
        modv = sb("modv", [128, 3, 3, 8], F32)
        gs = sb("gs", [128, 3, 8], F32)
        fg_bc = sb("fg_bc", [128, 1024], F32)
        dtb_bc = sb("dtb_bc", [128, NH], F32)
        A_bc = sb("A_bc", [128, NH], F32)
        dsk_bc = sb("dsk_bc", [128, NH], F32)
        wpool = sb("wpool", [128, 4, 128], BF16)
        sc = sb("sc", [128, 8], F32)
        sc_b = sb("sc_b", [128, 8, 128], BF16)
        xts = [sb("xt%d" % i, [128, MT, D], F32) for i in range(2)]
        yn = sb("yn", [128, MT, 1536], BF16)
        xn = yn[:].rearrange("p m f -> p (m f)")[:, 0:MT * D].rearrange("p (m f) -> p m f", m=MT)
        hT = sb("hT", [128, KC, T], BF16)
        gT = sb("gT", [128, NJ, T], BF16)
        ring = [sb("ring%d" % i, [128, SLOT_ELEMS], BF16) for i in range(NSLOT)]
        scr4 = [sb("scr%d" % i, [128, 528], F32) for i in range(4)]
        stat = sb("stat", [128, 16], F32)
        upool = sb("upool", [128, 4, 528], F32)
        pooledT = sb("pooledT", [128, 4, T], BF16)
        halo = sb("halo", [128, 20, 3], F32)
        sz = sb("sz", [128, MT, 1536], BF16)
        gate_bc = sz[:].rearrange("p m f -> p (m f)").bitcast(F32).rearrange("p (s d) -> p s d", s=3)
        dtc = sb("dtc", [128, 8, MT * NH], F32)
        rhsD = [sb("rhsD%d" % i, [128, 2, 768], BF16) for i in range(2)]
        avs = sb("avs", [128, 2, MT * NH], BF16)
        trib = sb("trib", [128, 2, 128], BF16)
        expD = [sb("expD%d" % i, [128, 6, 128], F32) for i in range(2)]
        MTt = [sb("MT%d" % i, [128, 6, 128], BF16) for i in range(2)]
        smk = [sb("smk%d" % i, [128, 4, 128], F32) for i in range(2)]
        xd = [sb("xd%d" % i, [128, 6, 64], BF16) for i in range(2)]
        xdd = [sb("xdd%d" % i, [128, 6, 64], BF16) for i in range(2)]
        xsD = [sb("xsD%d" % i, [128, 6, 64], F32) for i in range(2)]
        btok = [sb("btok%d" % i, [128, 128], BF16) for i in range(2)]
        stat2 = sb("stat2", [128, 2, 4], F32)
        yt = sb("yt", [128, NH, 64], F32)
        t1 = sb("t1", [128, 6, 64], F32)
        state = sb("state", [128, NH, 64], F32)
        state_bf = sb("state_bf", [128, NH, 64], BF16)
        ymT = sb("ymT", [128, 16, T], BF16)
        PS = st.enter_context(nc.psum_tensor("PS", [128, 4096], F32))

        def bank(b, n=512):
            return PS[:, b * 512:b * 512 + n]

        def bank_bf(hb):
            return PS[:, hb * 256:(hb + 1) * 256].bitcast(BF16)

        b_cst = Buf("cst"); b_ident = Buf("ident"); b_vecs = Buf("vecs"); b_modv = Buf("modv"); b_gs = Buf("gs")
        b_gate = [Buf("gate%d" % i) for i in range(3)]
        b_small = Buf("small")
        b_wpool = Buf("wpool"); b_sc = Buf("sc")
        b_xs = [[Buf("x%d_%d" % (i, m)) for m in range(MT)] for i in range(2)]
        b_xn = [Buf("xn%d" % m) for m in range(MT)]
        b_yn = [[Buf("yn%d_%d" % (m, g)) for g in range(4)] for m in range(MT)]
        all_yn = [b for row in b_yn for b in row]
        b_hT = [Buf("hT%d" % k) for k in range(KC)]
        b_gT = [Buf("gT%d" % j) for j in range(NJ)]
        b_ring = [Buf("ring%d" % i) for i in range(NSLOT)]
        b_scr = [Buf("scr%d" % i) for i in range(4)]
        b_stat = Buf("stat")
        b_up = [Buf("up%d" % g) for g in range(4)]
        b_pooled = [Buf("pooled%d" % g) for g in range(4)]
        b_halo = Buf("halo")
        b_sz = [Buf("sz%d" % m) for m in range(MT)]
        b_dtc = Buf("dtc"); b_avs = Buf("avs")
        b_rhsD = [Buf("rhsD0"), Buf("rhsD1")]; b_expD = [Buf("expD0"), Buf("expD1")]
        b_MT = [Buf("MT0"), Buf("MT1")]; b_smk = [Buf("smk0"), Buf("smk1")]
        b_xd = [Buf("xd0"), Buf("xd1")]; b_xdd = [Buf("xdd0"), Buf("xdd1")]; b_xsD = [Buf("xsD0"), Buf("xsD1")]
        b_btok = [Buf("btok0"), Buf("btok1")]; b_stat2 = [Buf("stat2_0"), Buf("stat2_1")]
        b_yt = [Buf("yt%d" % g) for g in range(4)]
        b_t1 = Buf("t1")
        b_state = [Buf("st%d" % g) for g in range(4)]
        b_statebf = [Buf("stbf%d" % g) for g in range(4)]
        b_ymT = [Buf("ymT%d" % k) for k in range(16)]
        b_bank = [Buf("bank%d" % i, excl=True) for i in range(8)]
        b_ps = [b_bank[i // 2] for i in range(16)]
        b_out = Buf("out")
        b_outm = [Buf("out%d" % m) for m in range(MT)]

        def pb(b):
            return [b_bank[b]]

        s_misc = P.new_dma_sem("misc")
        s_x = P.new_dma_sem("x")
        s_out = P.new_dma_sem("out")
        s_ring = [P.new_dma_sem("ring%d" % i) for i in range(NSLOT)]
        s_bb = [P.new_dma_sem("bb0"), P.new_dma_sem("bb1")]
        s_wada = [P.new_dma_sem("wada%d" % i) for i in range(NSLOT)]

        ring_pos = [0]

        def slot_view(si, A, Bc):
            return ring[si][:, 0:A * Bc].rearrange("p (a b) -> p a b", a=A)

        def load_unit(name):
            u = units[name]
            si = ring_pos[0] % NSLOT
            ring_pos[0] += 1
            v = slot_view(si, u["A"], u["Bc"])
            P.dma("sp", s_ring[si], lambda e, v=v, u=u: e.dma_start(out=v, in_=u["scr"]),
                  reads=[u["buf"]], writes=[b_ring[si]])
            return v, b_ring[si]

        P.dma("sp", s_misc, lambda e: e.dma_start(out=cst[:], in_=cst_d), writes=[b_cst])
        vrows = sb("vrows", [128, 128], F32)
        vrows2 = sb("vrows2", [20, 128], F32)
        b_vrows = Buf("vrows")
        P.dma("sp", s_misc, lambda e: e.dma_start(out=vrows[0:24, :], in_=ng_d.rearrange("s (kc p) -> (s kc) p", p=128)), writes=[b_vrows])
        P.dma("sp", s_misc, lambda e: e.dma_start(out=vrows[24:104, :], in_=convw_d.rearrange("k (cc p) -> (k cc) p", p=128)), writes=[b_vrows])
        P.dma("sp", s_misc, lambda e: e.dma_start(out=vrows[104:124, :], in_=convb_d.rearrange("(cc p) -> cc p", p=128)), writes=[b_vrows])
        P.dma("sp", s_misc, lambda e: e.dma_start(out=vrows[124:128, :], in_=pscale_d.rearrange("(g p) -> g p", p=128)), writes=[b_vrows])
        P.dma("sp", s_misc, lambda e: e.dma_start(out=vrows2[0:12, :], in_=ssmg_d.rearrange("(cc p) -> cc p", p=128)), writes=[b_vrows])
        P.dma("sp", s_misc, lambda e: e.dma_start(out=vrows2[12:20, :], in_=c_d.rearrange("(kc p) -> kc p", p=128)), writes=[b_vrows])
        P.dma("sp", s_misc, lambda e: e.dma_start(out=dtb_bc[:], in_=dtb_d.unsqueeze(0).broadcast_to([128, NH])), writes=[b_small])
        P.dma("sp", s_misc, lambda e: e.dma_start(out=A_bc[:], in_=alog_d.unsqueeze(0).broadcast_to([128, NH])), writes=[b_small])
        P.dma("sp", s_misc, lambda e: e.dma_start(out=dsk_bc[:], in_=dskip_d.unsqueeze(0).broadcast_to([128, NH])), writes=[b_small])
        P.dma("sp", s_misc, lambda e: e.dma_start(out=fg_bc[:], in_=fg_d.unsqueeze(0).broadcast_to([128, D])), writes=[b_small])
        s_wp = P.new_dma_sem("wp")
        P.dma("pool", s_wp, lambda e: e.dma_start(out=wpool[:], in_=wpool_d.rearrange("g c d -> c g d")), writes=[b_wpool])
        for b_ in (b_cst, b_vrows, b_small):
            b_.lw = (s_misc, P.dma_cnt[s_misc])

        s_xm = [[P.new_dma_sem("x%d_%d" % (i, m)) for m in range(MT)] for i in range(2)]
        s_om = [P.new_dma_sem("o%d" % m) for m in range(MT)]

        def load_x_m(t, m):
            xt_ = xts[t % 2]
            P.dma("pool", s_xm[t % 2][m], lambda e, t=t, m=m, xt_=xt_: e.dma_start(out=xt_[:, m, :], in_=x_d[t * T + m * 128:t * T + (m + 1) * 128, :]),
                  writes=[b_xs[t % 2][m]])

        def load_x(t):
            for m in range(MT):
                load_x_m(t, m)
        load_x(0)

        P.op("dve", lambda e: e.tensor_copy(out=ident[:], in_=identf), reads=[b_cst], writes=[b_ident])
        P.op("dve", lambda e: e.tensor_copy(out=trib[:].rearrange("p a b -> p (a b)"), in_=cst[:, 128:384]), reads=[b_cst], writes=[b_ident])
        P.op("pe", lambda e: e.matmul(bank(0, 128), lhsT=vrows[:], rhs=identf, start=True, stop=True),
             reads=[b_vrows, b_cst], writes=pb(0))
        P.op("pe", lambda e: e.matmul(bank(1, 20), lhsT=vrows2[:], rhs=identf[0:20, 0:20], start=True, stop=True),
             reads=[b_vrows, b_cst], writes=pb(1))
        P.op("dve", lambda e: e.tensor_copy(out=vecs1[:], in_=bank(0, 128)), reads=pb(0), writes=[b_vecs])
        P.op("dve", lambda e: e.tensor_copy(out=vecs2[:], in_=bank(1, 20)), reads=pb(1), writes=[b_vecs])
        ng_fm = vecs1[:, 0:24].rearrange("p (s k) -> p s k", s=3)
        convw_fm = vecs1[:, 24:104].rearrange("p (k c) -> p k c", k=4)
        convb_fm = vecs1[:, 104:124]
        pscale_fm = vecs1[:, 124:128]
        ssmg_fm = vecs2[:, 0:12]
        c_fm = vecs2[:, 12:20]
        P.op("act", lambda e: e.activation(out=A_bc[:], in_=A_bc[:], func=AF.Exp), reads=[b_small], writes=[b_small])
        P.op("dve", lambda e: e.tensor_scalar(out=A_bc[:], in0=A_bc[:], scalar1=-1.0, scalar2=None, op0=ALU.mult),
             reads=[b_small], writes=[b_small])
        P.op("act", lambda e: e.activation(out=sc[:], in_=c_fm, func=AF.Silu), reads=[b_vecs], writes=[b_sc])
        P.op("dve", lambda e: e.tensor_copy(out=sc_b[:], in_=sc[:].unsqueeze(2).broadcast_to([128, 8, 128])),
             reads=[b_sc], writes=[b_sc])
        P.op("pool", lambda e: e.memset(state[:], 0.0), writes=b_state)
        P.op("pool", lambda e: e.memset(state_bf[:], 0.0), writes=b_statebf)
        P.op("pool", lambda e: e.memset(halo[:], 0.0), writes=[b_halo])
        P.op("pool", lambda e: e.memset(upool[:], 0.0), writes=b_up)

        wav = wada_d.rearrange("(kc p) n -> p kc n", p=128)
        for blk in range(18):
            s_i, which, half = blk // 6, (blk % 6) // 2, blk % 2
            si = ring_pos[0] % NSLOT
            ring_pos[0] += 1
            v = slot_view(si, 8, 512)
            P.dma("pool", s_wada[si], lambda e, v=v, blk=blk: e.dma_start(out=v, in_=wav[:, :, blk * 512:(blk + 1) * 512]),
                  writes=[b_ring[si]])
            bb = scr4[blk % 2]
            P.dma("sp", s_bb[blk % 2], lambda e, bb=bb, blk=blk: e.dma_start(
                out=bb[:, 0:512], in_=bada_d[blk * 512:(blk + 1) * 512].unsqueeze(0).broadcast_to([128, 512])),
                writes=[b_scr[blk % 2]])
            bk = blk % 2
            P.op("pe", [(lambda e, kc=kc, v=v, bk=bk: e.matmul(bank(bk), lhsT=sc_b[:, kc, :], rhs=v[:, kc, :],
                                                                start=(kc == 0), stop=(kc == 7))) for kc in range(8)],
                 reads=[b_sc, b_ring[si]], writes=pb(bk))
            if which == 2:
                dst = gate_bc[:, s_i, half * 512:(half + 1) * 512]
                P.op("dve", lambda e, dst=dst, bk=bk, bb=bb: e.tensor_tensor(out=dst, in0=bank(bk), in1=bb[:, 0:512], op=ALU.add),
                     reads=pb(bk) + [b_scr[blk % 2]], writes=[b_gate[s_i]] + b_sz)
                if s_i != 1:
                    P.op("dve", lambda e, dst=dst: e.tensor_scalar(out=dst, in0=dst, scalar1=0.5, scalar2=None, op0=ALU.mult),
                         reads=[b_gate[s_i]], writes=[b_gate[s_i]] + b_sz)
            else:
                tmp = scr4[2 + blk % 2]
                P.op("dve", lambda e, tmp=tmp, bk=bk, bb=bb: e.tensor_tensor(out=tmp[:, 0:512], in0=bank(bk), in1=bb[:, 0:512], op=ALU.add),
                     reads=pb(bk) + [b_scr[blk % 2]], writes=[b_scr[2 + blk % 2]])
                P.op("dve", lambda e, tmp=tmp: e.tensor_tensor(
                    out=tmp[:, 0:512].rearrange("p (a b) -> p a b", a=4), in0=tmp[:, 0:512].rearrange("p (a b) -> p a b", a=4),
                    in1=identf.unsqueeze(1).broadcast_to([128, 4, 128]), op=ALU.mult),
                    reads=[b_scr[2 + blk % 2], b_cst], writes=[b_scr[2 + blk % 2]])
                P.op("dve", lambda e, tmp=tmp, s_i=s_i, which=which, half=half: e.tensor_reduce(
                    out=modv[:, s_i, which, half * 4:(half + 1) * 4], in_=tmp[:, 0:512].rearrange("p (a b) -> p a b", a=4),
                    axis=AX.X, op=ALU.add),
                    reads=[b_scr[2 + blk % 2]], writes=[b_modv])
        P.op("dve", lambda e: e.scalar_tensor_tensor(out=gs[:], in0=modv[:, :, 1, :], scalar=1.0, in1=ng_fm, op0=ALU.add, op1=ALU.mult),
             reads=[b_modv, b_vecs], writes=[b_gs])

        is_o = lambda n: "_o" in n
        pre_order = ([n for n in order_pre if n.startswith("f0_") and not is_o(n)] + [n for n in order_pre if n.startswith("f0_o")] +
                     [n for n in order_pre if n.startswith("m_o")] + [n for n in order_pre if n.startswith("f1_o")] +
                     [n for n in order_pre if n.startswith("m_") and not is_o(n)] + [n for n in order_pre if n.startswith("f1_") and not is_o(n)])
        assert sorted(pre_order) == sorted(order_pre)
        for name in pre_order:
            u = units[name]
            sname = P.new_dma_sem("pre_" + name)
            P.dma("pool", sname, lambda e, u=u: e.dma_start(out=u["scr"], in_=u["src"]), writes=[u["buf"]])

        s_gl = [P.new_dma_sem("gl0"), P.new_dma_sem("gl1")]
        s_gs = [P.new_dma_sem("gs0"), P.new_dma_sem("gs1")]
        gate_rot = [0]
        ymT_flat = ymT[:].rearrange("p k t -> p (k t)")
        gate_store_pending = [None]

        def gate_store_flush():
            if gate_store_pending[0] is not None:
                gate_store_pending[0]()
                gate_store_pending[0] = None

        def gate_unit(name, s_i):
            gate_store_flush()
            u = units[name]
            A = u["A"]
            gi_ = gate_rot[0] % 2
            gate_rot[0] += 1
            stg = ymT_flat[:, gi_ * 4096:gi_ * 4096 + A * 1024].rearrange("p (a b) -> p a b", a=A)
            sbufs = b_ymT[gi_ * 8:gi_ * 8 + 8]
            P.dma("sp", s_gl[gi_], lambda e, stg=stg, u=u: e.dma_start(out=stg, in_=u["scr"]), reads=[u["buf"]], writes=sbufs)
            P.op("dve", lambda e, stg=stg, s_i=s_i, A=A: e.tensor_tensor(out=stg, in0=stg,
                                                                        in1=gate_bc[:, s_i, :].unsqueeze(1).broadcast_to([128, A, 1024]), op=ALU.mult),
                 reads=[b_gate[s_i]] + b_sz, writes=sbufs)

            def store(stg=stg, u=u, gi_=gi_, sbufs=sbufs):
                P.dma("sp", s_gs[gi_], lambda e, stg=stg, u=u: e.dma_start(out=u["scr"], in_=stg), reads=sbufs, writes=[u["buf"]])
            gate_store_pending[0] = store

        gate_pending = ([("f0_o%d" % g, 0) for g in range(6)] + [("m_o%d" % g, 1) for g in range(4)] + [("f1_o%d" % g, 2) for g in range(6)])

        tp_rot = [0]

        cur = {"t": 0}

        def norm_to_hT(s_i):
            norm_part1(cur["t"])
            norm_part2(s_i)

        def norm_part1(t_idx):
            xt = xts[t_idx % 2]; b_x = b_xs[t_idx % 2]
            for m in range(MT):
                P.op("act", lambda e, m=m: e.activation(out=xn[:, m, :], in_=xt[:, m, :], func=AF.Square, scale=1.0 / 32.0,
                                                         accum_out=stat[:, m:m + 1]),
                     reads=[b_x[m]], writes=(b_xn + all_yn if m == 0 else [b_xn[m]]) + [b_stat])
            P.op("act", lambda e: e.activation(out=stat[:, 4:8], in_=stat[:, 0:4], func=AF.Sqrt, bias=EPS, scale=1.0),
                 reads=[b_stat], writes=[b_stat])
            P.op("dve", lambda e: e.reciprocal(out=stat[:, 8:12], in_=stat[:, 4:8]), reads=[b_stat], writes=[b_stat])
            for m in range(MT):
                P.op("dve", lambda e, m=m: e.tensor_scalar(out=xn[:, m, :], in0=xt[:, m, :], scalar1=stat[:, 8 + m:9 + m], scalar2=None,
                                                            op0=ALU.mult),
                     reads=[b_x[m], b_stat], writes=[b_xn[m]])

        def norm_part2(s_i):
            for kc in range(KC):
                hb = 8 + 2 * (tp_rot[0] % 4)
                tp_rot[0] += 1
                P.op("pe", [(lambda e, m=m, kc=kc, hb=hb: e.transpose(out=bank_bf(hb)[:, m * 128:(m + 1) * 128],
                                                                        in_=xn[:, m, kc * 128:(kc + 1) * 128], identity=ident[:]))
                            for m in range(MT)],
                     reads=b_xn + [b_ident], writes=[b_ps[hb]])
                eng = "act" if kc % 2 == 0 else "dve"
                if eng == "act":
                    P.op("act", lambda e, kc=kc, hb=hb: e.activation(out=hT[:, kc, :], in_=bank_bf(hb), func=AF.Identity,
                                                                      scale=gs[:, s_i, kc:kc + 1], bias=modv[:, s_i, 0, kc:kc + 1]),
                         reads=[b_ps[hb], b_gs, b_modv], writes=[b_hT[kc]])
                else:
                    P.op("dve", lambda e, kc=kc, hb=hb: e.tensor_scalar(out=hT[:, kc, :], in0=bank_bf(hb), scalar1=gs[:, s_i, kc:kc + 1],
                                                                         scalar2=modv[:, s_i, 0, kc:kc + 1], op0=ALU.mult, op1=ALU.add),
                         reads=[b_ps[hb], b_gs, b_modv], writes=[b_hT[kc]])

        def out_proj(unit_names, lhs, lhs_bufs, nk_total, gate_i):
            xt = xts[cur["t"] % 2]; b_x = b_xs[cur["t"] % 2]
            k = 0
            for name in unit_names:
                v, rb = load_unit(name)
                for kk in range(units[name]["A"]):
                    fns = []
                    for m in range(MT):
                        for n in range(2):
                            fns.append(lambda e, m=m, n=n, kk=kk, k=k, v=v: e.matmul(
                                bank(m * 2 + n), lhsT=lhs[:, k, m * 128:(m + 1) * 128], rhs=v[:, kk, n * 512:(n + 1) * 512],
                                start=(k == 0), stop=(k == nk_total - 1)))
                    P.op("pe", fns, reads=[rb, lhs_bufs[k]], writes=b_bank)
                    k += 1
            for m in range(MT):
                for n in range(2):
                    xs_ = xt[:, m, n * 512:(n + 1) * 512]
                    P.op("dve", lambda e, m=m, n=n, xs_=xs_: e.tensor_tensor(out=xs_, in0=bank(m * 2 + n), in1=xs_, op=ALU.add),
                         reads=pb(m * 2 + n) + [b_x[m]], writes=[b_x[m]])

        def ffn(f, gate_i, pre_out=None):
            va = vb = None
            for j in range(NJ):
                g, jo = j // 4, j % 4
                if jo == 0:
                    va, ra = load_unit("f%d_a%d" % (f, g))
                    vb, rb = load_unit("f%d_b%d" % (f, g))
                ba, bb_ = 2 * (j % 2), 2 * (j % 2) + 1
                P.op("pe", [(lambda e, kc=kc, va=va, jo=jo, ba=ba: e.matmul(bank(ba), lhsT=va[:, kc, jo * 128:(jo + 1) * 128], rhs=hT[:, kc, :],
                                                                           start=(kc == 0), stop=(kc == 7))) for kc in range(8)],
                     reads=[ra] + b_hT, writes=pb(ba))
                P.op("pe", [(lambda e, kc=kc, vb=vb, jo=jo, bb_=bb_: e.matmul(bank(bb_), lhsT=vb[:, kc, jo * 128:(jo + 1) * 128], rhs=hT[:, kc, :],
                                                                             start=(kc == 0), stop=(kc == 7))) for kc in range(8)],
                     reads=[rb] + b_hT, writes=pb(bb_))
                if gate_pending and j >= 6:
                    gate_unit(*gate_pending.pop(0))
                elif j >= 6:
                    gate_store_flush()
                sa = scr4[2 + j % 2]
                P.op("act", lambda e, sa=sa, ba=ba: e.activation(out=sa[:, 0:512], in_=bank(ba), func=AF.Silu),
                     reads=pb(ba), writes=[b_scr[2 + j % 2]])
                P.op("dve", lambda e, sa=sa, bb_=bb_, j=j: e.tensor_tensor(out=gT[:, j, :], in0=sa[:, 0:512], in1=bank(bb_), op=ALU.mult),
                     reads=pb(bb_) + [b_scr[2 + j % 2]], writes=[b_gT[j]])
            gate_store_flush()
            assert not gate_pending
            if pre_out is not None:
                pre_out()
            out_proj(["f%d_o%d" % (f, g) for g in range(6)], gT, b_gT, NJ, gate_i)

        rot = {"fm": 0}

        def mixer(t):
            v, rb = load_unit("m_up")
            for gi in range(4):
                bk = rot["fm"] % 4; rot["fm"] += 1
                P.op("pe", [(lambda e, kc=kc, v=v, gi=gi, bk=bk: e.matmul(bank(bk), lhsT=v[:, kc, gi * 128:(gi + 1) * 128], rhs=hT[:, kc, :],
                                                                         start=(kc == 0), stop=(kc == 7))) for kc in range(8)],
                     reads=[rb] + b_hT, writes=pb(bk))
                P.op("act", lambda e, gi=gi, bk=bk: e.copy(out=upool[:, gi, 16:528], in_=bank(bk)), reads=pb(bk), writes=[b_up[gi]])
            conv_pending = [None]

            def conv_unit(name, cc0):
                v, rb = load_unit(name)
                for q in range(4):
                    cc = cc0 + q
                    bk = rot["fm"] % 4; rot["fm"] += 1
                    P.op("pe", [(lambda e, kc=kc, v=v, q=q, bk=bk: e.matmul(bank(bk), lhsT=v[:, kc, q * 128:(q + 1) * 128], rhs=hT[:, kc, :],
                                                                           start=(kc == 0), stop=(kc == 7))) for kc in range(8)],
                         reads=[rb] + b_hT, writes=pb(bk))
                    raw = scr4[cc % 2]; acc = scr4[2 + cc % 2]
                    braw = b_scr[cc % 2]; bacc = b_scr[2 + cc % 2]
                    P.op("act", lambda e, raw=raw, bk=bk: e.copy(out=raw[:, 3:515], in_=bank(bk)), reads=pb(bk), writes=[braw])
                    P.op("act", lambda e, acc=acc, bk=bk, cc=cc: e.activation(out=acc[:, 0:512], in_=bank(bk), func=AF.Identity,
                                                                              scale=convw_fm[:, 3, cc:cc + 1], bias=convb_fm[:, cc:cc + 1]),
                         reads=pb(bk) + [b_vecs], writes=[bacc])
                    P.op("act", lambda e, raw=raw, cc=cc: e.copy(out=raw[:, 0:3], in_=halo[:, cc, :]), reads=[b_halo], writes=[braw])
                    if conv_pending[0] is not None:
                        conv_pending[0]()
                    for k in range(3):
                        P.op("dve", lambda e, raw=raw, acc=acc, cc=cc, k=k: e.scalar_tensor_tensor(
                            out=acc[:, 0:512], in0=raw[:, k:k + 512], scalar=convw_fm[:, k, cc:cc + 1], in1=acc[:, 0:512],
                            op0=ALU.mult, op1=ALU.add), reads=[braw, bacc, b_vecs], writes=[bacc])
                    P.op("pool", lambda e, raw=raw, cc=cc: e.tensor_copy(out=halo[:, cc, :], in_=raw[:, 512:515]), reads=[braw], writes=[b_halo])

                    def fin(acc=acc, cc=cc, bacc=bacc):
                        P.op("act", lambda e, acc=acc, cc=cc: e.activation(out=gT[:, cc, :], in_=acc[:, 0:512], func=AF.Silu),
                             reads=[bacc], writes=[b_gT[cc]])
                    conv_pending[0] = fin

            def conv_flush():
                if conv_pending[0] is not None:
                    conv_pending[0]()
                    conv_pending[0] = None
            conv_unit("m_B", 12)
            conv_unit("m_C", 16)
            v, rb = load_unit("m_dt")
            pdt = bank(6, 96)
            P.op("pe", [(lambda e, kc=kc, m=m, v=v: e.matmul(bank(6)[:, m * NH:(m + 1) * NH], lhsT=hT[:, kc, m * 128:(m + 1) * 128], rhs=v[:, kc, :],
                                                              start=(kc == 0), stop=(kc == 7))) for m in range(MT) for kc in range(8)],
                 reads=[rb] + b_hT, writes=pb(6))
            dtr, dtv, av, csv, dec_out, cdec, dst_, dts = [dtc[:, i, :] for i in range(8)]
            m24 = lambda ap: ap.rearrange("p (m h) -> p m h", m=MT)
            P.op("dve", lambda e: e.tensor_tensor(out=m24(dtr), in0=m24(pdt), in1=dtb_bc[:].unsqueeze(1).broadcast_to([128, MT, NH]), op=ALU.add),
                 reads=pb(6) + [b_small], writes=[b_dtc])
            P.op("act", lambda e: e.activation(out=dtr, in_=dtr, func=AF.Exp), reads=[b_dtc], writes=[b_dtc])
            P.op("act", lambda e: e.activation(out=dtv, in_=dtr, func=AF.Ln, bias=1.0), reads=[b_dtc], writes=[b_dtc])
            P.op("dve", lambda e: e.tensor_tensor(out=m24(av), in0=m24(dtv), in1=A_bc[:].unsqueeze(1).broadcast_to([128, MT, NH]), op=ALU.mult),
                 reads=[b_dtc, b_small], writes=[b_dtc])
            P.op("dve", lambda e: e.tensor_copy(out=avs[:, 0, :], in_=av), reads=[b_dtc], writes=[b_avs])
            P.op("dve", lambda e: e.tensor_tensor(out=avs[:, 1, :], in0=av, in1=avs[:, 0, :], op=ALU.subtract), reads=[b_dtc, b_avs], writes=[b_avs])
            P.op("pe", lambda e: e.matmul(bank(6)[:, 128:224], lhsT=tri, rhs=av, start=True, stop=True), reads=[b_dtc, b_cst], writes=pb(6))
            P.op("pe", lambda e: e.matmul(bank(6)[:, 256:352], lhsT=ones, rhs=av, start=True, stop=True), reads=[b_dtc, b_cst], writes=pb(6))
            P.op("act", lambda e: e.copy(out=csv, in_=bank(6)[:, 128:224]), reads=pb(6), writes=[b_dtc])
            P.op("act", lambda e: e.activation(out=dec_out, in_=bank(6)[:, 128:224], func=AF.Exp), reads=pb(6), writes=[b_dtc])
            P.op("act", lambda e: e.activation(out=cdec, in_=bank(6)[:, 256:352], func=AF.Exp), reads=pb(6), writes=[b_dtc])
            P.op("dve", lambda e: e.tensor_tensor(out=dst_, in0=bank(6)[:, 256:352], in1=csv, op=ALU.subtract), reads=pb(6) + [b_dtc], writes=[b_dtc])
            P.op("act", lambda e: e.activation(out=dst_, in_=dst_, func=AF.Exp), reads=[b_dtc], writes=[b_dtc])
            P.op("dve", lambda e: e.tensor_tensor(out=dts, in0=dst_, in1=dtv, op=ALU.mult), reads=[b_dtc], writes=[b_dtc])
            for i in range(3):
                conv_unit("m_xs%d" % i, i * 4)
            conv_flush()
            for i in range(3):
                v, rb = load_unit("m_z%d" % i)
                for m in range(MT):
                    bk = rot["fm"] % 4; rot["fm"] += 1
                    P.op("pe", [(lambda e, kc=kc, v=v, m=m, bk=bk: e.matmul(bank(bk), lhsT=hT[:, kc, m * 128:(m + 1) * 128], rhs=v[:, kc, :],
                                                                           start=(kc == 0), stop=(kc == 7))) for kc in range(8)],
                         reads=[rb] + b_hT, writes=pb(bk))
                    P.op("act", lambda e, m=m, i=i, bk=bk: e.activation(out=sz[:, m, i * 512:(i + 1) * 512], in_=bank(bk), func=AF.Silu),
                         reads=pb(bk), writes=[b_sz[m]])
            for gi in range(4):
                cur = upool[:, gi, :]
                curb = b_up[gi]
                w = 1
                for lvl in range(gi + 1):
                    dst = scr4[lvl % 2]; dstb = b_scr[lvl % 2]
                    lo = 2 * w - 1
                    P.op("pool", lambda e, cur=cur, dst=dst, w=w, lo=lo: e.tensor_tensor(out=dst[:, lo:528], in0=cur[:, lo:528],
                                                                                       in1=cur[:, lo - w:528 - w], op=ALU.add),
                         reads=[curb], writes=[dstb])
                    cur = dst[:]; curb = dstb; w *= 2
                P.op("dve", lambda e, cur=cur, gi=gi, w=w: e.scalar_tensor_tensor(out=pooledT[:, gi, :], in0=cur[:, 16:528], scalar=1.0 / w,
                                                                                 in1=upool[:, gi, 16:528], op0=ALU.mult, op1=ALU.subtract),
                     reads=[curb, b_up[gi]], writes=[b_pooled[gi]])
                if t == 0:
                    tmpc = scr4[2]
                    P.op("dve", lambda e, cur=cur, gi=gi, tmpc=tmpc: e.tensor_tensor(out=tmpc[:, 0:16], in0=cur[:, 16:32],
                                                                                    in1=invcnt[:, gi * 16:(gi + 1) * 16], op=ALU.mult),
                         reads=[curb, b_cst], writes=[b_scr[2]])
                    P.op("dve", lambda e, gi=gi, tmpc=tmpc: e.tensor_tensor(out=pooledT[:, gi, 0:16], in0=tmpc[:, 0:16],
                                                                           in1=upool[:, gi, 16:32], op=ALU.subtract),
                         reads=[b_scr[2], b_up[gi]], writes=[b_pooled[gi]])
                P.op("pool", lambda e, gi=gi: e.tensor_copy(out=upool[:, gi, 0:16], in_=upool[:, gi, 512:528]), reads=[b_up[gi]], writes=[b_up[gi]])
                bk = rot["fm"] % 4; rot["fm"] += 1
                P.op("pe", lambda e, gi=gi, bk=bk: e.matmul(bank(bk), lhsT=wpool[:, gi, :], rhs=pooledT[:, gi, :], start=True, stop=True),
                     reads=[b_wpool, b_pooled[gi]], writes=pb(bk))
                P.op("act", lambda e, gi=gi, bk=bk: e.activation(out=ymT[:, gi, :], in_=bank(bk), func=AF.Identity, scale=pscale_fm[:, gi:gi + 1]),
                     reads=pb(bk) + [b_vecs], writes=[b_ymT[gi]])
            xsT = lambda cc, m: gT[:, cc, m * 128:(m + 1) * 128]
            BTm = lambda g, m: gT[:, 12 + g, m * 128:(m + 1) * 128]
            CTm = lambda g, m: gT[:, 16 + g, m * 128:(m + 1) * 128]
            h24 = lambda ap, m: ap.rearrange("p (m h) -> p m h", m=MT)[:, m, :]
            psx = PS[:, 4 * 512:4 * 512 + 768].bitcast(BF16)
            psb = bank_bf(11)
            def prep_s(m):
                i2 = m % 2
                P.op("pe", [(lambda e, g=g, m=m: e.matmul(bank(6)[:, g * 128:(g + 1) * 128], lhsT=BTm(g, m), rhs=CTm(g, m), start=True, stop=True))
                            for g in range(4)], reads=b_gT[12:20], writes=pb(6))
                P.op("dve", lambda e, i2=i2: e.tensor_tensor(out=smk[i2][:], in0=bank(6).rearrange("p (g l) -> p g l", g=4),
                                                             in1=tri.unsqueeze(1).broadcast_to([128, 4, 128]), op=ALU.mult),
                     reads=pb(6) + [b_cst], writes=[b_smk[i2]])

            bc6 = lambda ap: ap.unsqueeze(2).broadcast_to([128, 6, 64])

            def a1(m, g, ig):
                hs = slice(g * 6, (g + 1) * 6)
                for part in range(2):
                    P.op("dve", lambda e, ig=ig, m=m, hs=hs, part=part: e.tensor_tensor(
                        out=rhsD[ig][:, part, :].rearrange("p (r l) -> p r l", r=6),
                        in0=h24(avs[:, part, :], m)[:, hs].unsqueeze(2).broadcast_to([128, 6, 128]),
                        in1=trib[:, 0, :].unsqueeze(1).broadcast_to([128, 6, 128]), op=ALU.mult),
                        reads=[b_avs, b_ident], writes=[b_rhsD[ig]])
                P.op("pe", [lambda e, ig=ig: e.matmul(bank(0), lhsT=trib[:, 1, :], rhs=rhsD[ig][:, 0, 0:512], start=True, stop=False),
                            lambda e, ig=ig: e.matmul(bank(0), lhsT=trib[:, 1, :], rhs=rhsD[ig][:, 1, 0:512], start=False, stop=True),
                            lambda e, ig=ig: e.matmul(bank(1, 256), lhsT=trib[:, 1, :], rhs=rhsD[ig][:, 0, 512:768], start=True, stop=False),
                            lambda e, ig=ig: e.matmul(bank(1, 256), lhsT=trib[:, 1, :], rhs=rhsD[ig][:, 1, 512:768], start=False, stop=True)],
                     reads=[b_rhsD[ig], b_ident], writes=pb(0) + pb(1))
                P.op("act", lambda e, ig=ig: e.activation(out=expD[ig][:].rearrange("p r l -> p (r l)"), in_=PS[:, 0:768], func=AF.Exp),
                     reads=pb(0) + pb(1), writes=[b_expD[ig]])

            def a2_pe(m, g, ig):
                bk = 4 + ig
                pst = bank(bk).bitcast(BF16)
                P.op("pe", [(lambda e, q=q, m=m, g=g, pst=pst: e.transpose(out=pst[:, q * 128:(q + 1) * 128], in_=xsT(3 * g + q, m), identity=ident[:]))
                            for q in range(3)] +
                           [lambda e, m=m, g=g, pst=pst: e.transpose(out=pst[:, 384:512], in_=BTm(g, m), identity=ident[:])],
                     reads=b_gT[3 * g:3 * g + 3] + [b_gT[12 + g], b_ident], writes=pb(bk))

            def a2_rest(m, g, ig):
                i2 = m % 2
                hs = slice(g * 6, (g + 1) * 6)
                bk = 4 + ig
                pst = bank(bk).bitcast(BF16)
                px = pst[:, 0:384].rearrange("p (h d) -> p h d", h=6)
                P.op("dve", lambda e, m=m, hs=hs, ig=ig, px=px: e.tensor_tensor(out=xd[ig][:], in0=px, in1=bc6(h24(dtv, m)[:, hs]), op=ALU.mult),
                     reads=pb(bk) + [b_dtc], writes=[b_xd[ig]])
                P.op("dve", lambda e, m=m, hs=hs, ig=ig, px=px: e.tensor_tensor(out=xdd[ig][:], in0=px, in1=bc6(h24(dts, m)[:, hs]), op=ALU.mult),
                     reads=pb(bk) + [b_dtc], writes=[b_xdd[ig]])
                P.op("dve", lambda e, hs=hs, ig=ig, px=px: e.tensor_tensor(out=xsD[ig][:], in0=px, in1=bc6(dsk_bc[:, hs]), op=ALU.mult),
                     reads=pb(bk) + [b_small], writes=[b_xsD[ig]])
                P.op("act", lambda e, ig=ig, pst=pst: e.copy(out=btok[ig][:], in_=pst[:, 384:512]), reads=pb(bk), writes=[b_btok[ig]])
                P.op("pool", lambda e, ig=ig, i2=i2, g=g: e.tensor_tensor(
                    out=MTt[ig][:], in0=expD[ig][:], in1=smk[i2][:, g, :].unsqueeze(1).broadcast_to([128, 6, 128]), op=ALU.mult),
                    reads=[b_expD[ig], b_smk[i2]], writes=[b_MT[ig]])

            def state_decay(m, g):
                hs = slice(g * 6, (g + 1) * 6)
                stg = state[:, hs, :]
                P.op("pool", lambda e, stg=stg, m=m, hs=hs: e.tensor_tensor(
                    out=stg, in0=stg, in1=h24(cdec, m)[:, hs].unsqueeze(2).broadcast_to([128, 6, 64]), op=ALU.mult),
                    reads=[b_state[g], b_dtc], writes=[b_state[g]])

            def b_pe(m, g, ig):
                hs = slice(g * 6, (g + 1) * 6)
                P.op("pe", [(lambda e, r=r, ig=ig: e.matmul(bank(2)[:, r * 64:(r + 1) * 64], lhsT=MTt[ig][:, r, :], rhs=xd[ig][:, r, :],
                                                            start=True, stop=True)) for r in range(6)],
                     reads=[b_MT[ig], b_xd[ig]], writes=pb(2))
                sbf = state_bf[:, hs, :].rearrange("p h d -> p (h d)")
                P.op("pe", lambda e, g=g, m=m, sbf=sbf: e.matmul(bank(3, 384), lhsT=CTm(g, m), rhs=sbf, start=True, stop=True),
                     reads=b_gT[16:20] + [b_statebf[g]], writes=pb(3))
                xddg = xdd[ig][:].rearrange("p h d -> p (h d)")
                P.op("pe", lambda e, ig=ig, xddg=xddg: e.matmul(bank(7, 384), lhsT=btok[ig][:], rhs=xddg, start=True, stop=True),
                     reads=[b_btok[ig], b_xdd[ig]], writes=pb(7))

            def b_dve(m, g, ig):
                hs = slice(g * 6, (g + 1) * 6)
                stg = state[:, hs, :]
                ytg = yt[:, hs, :]
                P.op("dve", lambda e, ytg=ytg, ig=ig: e.tensor_tensor(out=ytg, in0=bank(2, 384).rearrange("p (h d) -> p h d", h=6),
                                                                     in1=xsD[ig][:], op=ALU.add),
                     reads=pb(2) + [b_xsD[ig]], writes=[b_yt[g]])
                P.op("dve", lambda e, m=m, hs=hs: e.tensor_tensor(out=t1[:], in0=bank(3, 384).rearrange("p (h d) -> p h d", h=6),
                                                                  in1=h24(dec_out, m)[:, hs].unsqueeze(2).broadcast_to([128, 6, 64]), op=ALU.mult),
                     reads=pb(3) + [b_dtc], writes=[b_t1])
                P.op("dve", lambda e, stg=stg: e.tensor_tensor(out=stg, in0=stg, in1=bank(7, 384).rearrange("p (h d) -> p h d", h=6), op=ALU.add),
                     reads=pb(7) + [b_state[g]], writes=[b_state[g]])
                P.op("act", lambda e, stg=stg, hs=hs: e.copy(out=state_bf[:, hs, :], in_=stg), reads=[b_state[g]], writes=[b_statebf[g]])

            def b_tail(m, g, ig):
                hs = slice(g * 6, (g + 1) * 6)
                ytg = yt[:, hs, :]
                ytgf = ytg.rearrange("p h d -> p (h d)")
                P.op("pool", lambda e, ytg=ytg: e.tensor_tensor(out=ytg, in0=ytg, in1=t1[:], op=ALU.add),
                     reads=[b_t1, b_yt[g]], writes=[b_yt[g]])
                P.op("pool", lambda e, ytgf=ytgf, m=m, g=g: e.tensor_tensor(out=ytgf, in0=ytgf, in1=sz[:, m, g * 384:(g + 1) * 384], op=ALU.mult),
                     reads=[b_yt[g], b_sz[m]], writes=[b_yt[g]])
                sq = stat2[:, ig, 0:1]; ln_ = stat2[:, ig, 1:2]; rs = stat2[:, ig, 2:3]
                P.op("act", lambda e, ytgf=ytgf, m=m, g=g, sq=sq: e.activation(out=yn[:, m, g * 384:(g + 1) * 384], in_=ytgf, func=AF.Square,
                                                                             scale=float(1.0 / np.sqrt(384.0)), accum_out=sq),
                     reads=[b_yt[g]], writes=[b_yn[m][g], b_stat2[ig]] + (b_xn if (m == 0 and g == 0) else []))
                P.op("act", lambda e, sq=sq, ln_=ln_: e.activation(out=ln_, in_=sq, func=AF.Ln, bias=EPS, scale=1.0),
                     reads=[b_stat2[ig]], writes=[b_stat2[ig]])
                P.op("act", lambda e, ln_=ln_, rs=rs: e.activation(out=rs, in_=ln_, func=AF.Exp, scale=-0.5),
                     reads=[b_stat2[ig]], writes=[b_stat2[ig]])
                P.op("act", lambda e, ytgf=ytgf, m=m, g=g, rs=rs: e.activation(out=yn[:, m, g * 384:(g + 1) * 384], in_=ytgf, func=AF.Identity, scale=rs),
                     reads=[b_yt[g], b_stat2[ig]], writes=[b_yn[m][g]])

            items = [(m, g) for m in range(MT) for g in range(4)]
            NI = len(items)
            done_s = set()

            def emit_a1(i):
                m, g = items[i]
                if m not in done_s:
                    prep_s(m)
                    done_s.add(m)
                a1(m, g, i % 2)
            for i in range(2):
                emit_a1(i)
                a2_pe(*items[i], i % 2)
                a2_rest(*items[i], i % 2)
            state_decay(*items[0])
            for i, (m, g) in enumerate(items):
                ig = i % 2
                if i + 2 < NI:
                    emit_a1(i + 2)
                b_pe(m, g, ig)
                if i + 2 < NI:
                    a2_pe(*items[i + 2], ig)
                b_dve(m, g, ig)
                if i + 1 < NI:
                    state_decay(*items[i + 1])
                b_tail(m, g, ig)
                if i + 2 < NI:
                    a2_rest(*items[i + 2], ig)
            for cc in range(12):
                hb = 8 + 2 * (tp_rot[0] % 4)
                tp_rot[0] += 1
                P.op("pe", [(lambda e, m=m, cc=cc, hb=hb: e.transpose(out=bank_bf(hb)[:, m * 128:(m + 1) * 128],
                                                                        in_=yn[:, m, cc * 128:(cc + 1) * 128], identity=ident[:]))
                            for m in range(MT)], reads=[b_yn[m][cc // 3] for m in range(MT)] + [b_ident], writes=[b_ps[hb]])
                if cc % 2 == 0:
                    P.op("act", lambda e, cc=cc, hb=hb: e.activation(out=ymT[:, 4 + cc, :], in_=bank_bf(hb), func=AF.Identity,
                                                                      scale=ssmg_fm[:, cc:cc + 1]),
                         reads=[b_ps[hb], b_vecs], writes=[b_ymT[4 + cc]])
                else:
                    P.op("dve", lambda e, cc=cc, hb=hb: e.tensor_scalar(out=ymT[:, 4 + cc, :], in0=bank_bf(hb), scalar1=ssmg_fm[:, cc:cc + 1],
                                                                         scalar2=None, op0=ALU.mult),
                         reads=[b_ps[hb], b_vecs], writes=[b_ymT[4 + cc]])
            out_proj(["m_o%d" % g for g in range(4)], ymT, b_ymT, 16, 1)

        def final_and_store(t, do_norm):
            xt = xts[t % 2]; b_x = b_xs[t % 2]
            if do_norm:
                ostage = gT[:].rearrange("p j t -> p (j t)")[:, 0:8192].bitcast(F32).rearrange("p (m d) -> p m d", m=MT)
                for m in range(MT):
                    P.op("act", lambda e, m=m: e.activation(out=ostage[:, m, :], in_=xt[:, m, :], func=AF.Square, scale=1.0 / 32.0,
                                                             accum_out=stat[:, m:m + 1]),
                         reads=[b_x[m]], writes=b_gT[4 * m:4 * m + 4] + [b_stat])
                P.op("act", lambda e: e.activation(out=stat[:, 4:8], in_=stat[:, 0:4], func=AF.Sqrt, bias=EPS, scale=1.0),
                     reads=[b_stat], writes=[b_stat])
                P.op("dve", lambda e: e.reciprocal(out=stat[:, 8:12], in_=stat[:, 4:8]), reads=[b_stat], writes=[b_stat])
                toks = []
                for m in range(MT):
                    P.op("dve", lambda e, m=m: e.scalar_tensor_tensor(out=ostage[:, m, :], in0=xt[:, m, :], scalar=stat[:, 8 + m:9 + m],
                                                                     in1=fg_bc[:], op0=ALU.mult, op1=ALU.mult),
                         reads=[b_x[m], b_stat, b_small], writes=b_gT[4 * m:4 * m + 4])
                    toks.append(P.dma("sp", s_om[m], lambda e, t=t, m=m: e.dma_start(out=out_d[t * T + m * 128:t * T + (m + 1) * 128, :],
                                                                                   in_=ostage[:, m, :]),
                                      reads=b_gT[4 * m:4 * m + 4], writes=[b_outm[m]]))
                return toks
            toks = []
            for m in range(MT):
                toks.append(P.dma("sp", s_om[m], lambda e, t=t, m=m: e.dma_start(out=out_d[t * T + m * 128:t * T + (m + 1) * 128, :], in_=xt[:, m, :]),
                                  reads=[b_x[m]], writes=[b_outm[m]]))
            return toks

        last = None
        for t in range(ntiles):
            cur["t"] = t
            if t + 1 < ntiles:
                load_x(t + 1)
            if stage >= 1:
                if t == 0 or stage < 3:
                    norm_to_hT(0)
                else:
                    norm_part2(0)
                if stage != 11:
                    ffn(0, 0)
            if stage >= 2:
                norm_to_hT(1)
                mixer(t)
            if stage >= 3:
                norm_to_hT(2)
                ffn(1, 2, pre_out=(lambda t=t: norm_part1(t + 1)) if t + 1 < ntiles else None)
            last = final_and_store(t, stage >= 3)
        P.wait_all("sp", last)
        P.emit()
    return nc


def make_consts():
    cst = np.zeros((128, 576), np.float32)
    cst[:, 0:128] = np.eye(128, dtype=np.float32)
    k = np.arange(128)
    cst[:, 128:256] = (k[:, None] <= k[None, :]).astype(np.float32)
    cst[:, 256:384] = (k[:, None] > k[None, :]).astype(np.float32)
    cst[:, 384:512] = 1.0
    tt = np.arange(16)
    for gi, w in enumerate((2, 4, 8, 16)):
        cst[:, 512 + gi * 16:512 + (gi + 1) * 16] = (1.0 / np.minimum(tt + 1, w)).astype(np.float32)[None, :]
    return cst


_CACHE = {}


def make_in_maps(inputs, ncores=8):
    f = lambda a: np.ascontiguousarray(np.asarray(a, dtype=np.float32))
    shared = {
        "w_ada": f(inputs["w_ada"][0]), "b_ada": f(inputs["b_ada"][0]), "norm_g": f(inputs["norm_g"][0]),
        "ffn1_in": f(inputs["ffn1_in"][0]), "ffn1_out": f(inputs["ffn1_out"][0]), "w_in": f(inputs["w_in"][0]),
        "w_pool": f(inputs["w_pool"][0]), "pool_scale": f(inputs["pool_scale"][0]), "conv_w": f(inputs["conv_w"][0]),
        "conv_b": f(inputs["conv_b"][0]), "dt_bias": f(inputs["dt_bias"][0]), "a_log": f(inputs["a_log"][0]),
        "d_skip": f(inputs["d_skip"][0]), "ssm_norm_g": f(inputs["ssm_norm_g"][0]), "w_out": f(inputs["w_out"][0]),
        "ffn2_in": f(inputs["ffn2_in"][0]), "ffn2_out": f(inputs["ffn2_out"][0]), "final_g": f(inputs["final_g"]),
        "cst": make_consts(),
    }
    x = np.asarray(inputs["x"], dtype=np.float32)
    c = np.asarray(inputs["c"], dtype=np.float32)
    maps = []
    for b in range(ncores):
        m = dict(shared)
        m["x"] = np.ascontiguousarray(x[b])
        m["c"] = np.ascontiguousarray(c[b])
        maps.append(m)
    return maps


def kernel(**inputs):
    if "nc" not in _CACHE:
        _CACHE["nc"] = build_program()
    nc = _CACHE["nc"]
    maps = make_in_maps(inputs, 8)
    res = run_bass_kernel_spmd(nc, maps, core_ids=list(range(8)))
    return np.stack([np.asarray(r["out"], dtype=np.float32) for r in res.results], axis=0)
```
